# Optimizing a Trainium2 kernel written in Bass

```python
import jax, jax.numpy as jnp
from jax import lax
import numpy as np

D_MODEL = 2048
BATCH = 2
SEQ = 4096
DEPTH = 2

CHUNK = 64
Q_BLOCK = 128
N_A = DEPTH // 2
N_B = DEPTH - N_A
PLE_DIM = 256
RMS_EPS = 1e-6
NEG_INF = -1e30

SB_HEAD_DIM = 128
SB_HEADS = D_MODEL // SB_HEAD_DIM
SB_WIDTH = SB_HEADS * SB_HEAD_DIM

QK_NOPE_DIM = 128
QK_ROPE_DIM = 64
V_HEAD_DIM = 128
MLA_HEADS = D_MODEL // V_HEAD_DIM
Q_LORA_RANK = D_MODEL // 4
KV_LORA_RANK = D_MODEL // 4
MLA_WIDTH = MLA_HEADS * V_HEAD_DIM
QK_HEAD_DIM = QK_NOPE_DIM + QK_ROPE_DIM
ROPE_THETA = 10000.0

kernel_name = "yoco_stickbreak_mla_hybrid"


def rms_norm(x, g):
    xf = x.astype(jnp.float32)
    y = xf * lax.rsqrt(jnp.mean(xf * xf, axis=-1, keepdims=True) + RMS_EPS)
    return (y * g.astype(jnp.float32)).astype(x.dtype)


def rope_tables(seq_len):
    inv = 1.0 / (ROPE_THETA ** (jnp.arange(0, QK_ROPE_DIM, 2, dtype=jnp.float32) / QK_ROPE_DIM))
    ang = jnp.arange(seq_len, dtype=jnp.float32)[:, None] * inv[None, :]
    return jnp.cos(ang), jnp.sin(ang)


def apply_rope(t, cos, sin):
    tf = t.astype(jnp.float32)
    t1, t2 = jnp.split(tf, 2, axis=-1)
    return jnp.concatenate([t1 * cos - t2 * sin, t1 * sin + t2 * cos], axis=-1).astype(t.dtype)


def to_blocks(t):
    b, s = t.shape[:2]
    t = t.reshape((b, s // Q_BLOCK, Q_BLOCK) + t.shape[2:])
    return jnp.moveaxis(t, 1, 0)


def from_blocks(t):
    t = jnp.moveaxis(t, 0, 1)
    return t.reshape((t.shape[0], t.shape[1] * t.shape[2]) + t.shape[3:])


def stick_breaking_attention(q, k, v):
    s_len, dh = q.shape[1], q.shape[-1]
    scale = dh ** -0.5
    kpos = jnp.arange(s_len)

    def one_block(args):
        i, qi = args
        qpos = i * Q_BLOCK + jnp.arange(Q_BLOCK)
        z = jnp.einsum('bqhd,bkhd->bhqk', qi, k, preferred_element_type=jnp.float32) * scale
        strict = kpos[None, :] < qpos[:, None]
        log_keep = jnp.where(strict, jax.nn.log_sigmoid(-z), 0.0)
        later = lax.cumsum(log_keep, axis=3, reverse=True) - log_keep
        w = jnp.where(strict, jnp.exp(jax.nn.log_sigmoid(z) + later), 0.0)
        return jnp.einsum('bhqk,bkhd->bqhd', w.astype(v.dtype), v)

    out = lax.map(one_block, (jnp.arange(s_len // Q_BLOCK), to_blocks(q)))
    return from_blocks(out)


def mla_attention(q_nope, q_rope, k_nope, k_rope, v):
    s_len = q_nope.shape[1]
    scale = QK_HEAD_DIM ** -0.5
    kchunk = jnp.arange(s_len) // CHUNK

    def one_block(args):
        i, qn, qr = args
        qchunk = (i * Q_BLOCK + jnp.arange(Q_BLOCK)) // CHUNK
        s = (jnp.einsum('bqhd,bkhd->bhqk', qn, k_nope, preferred_element_type=jnp.float32)
             + jnp.einsum('bqhr,bkr->bhqk', qr, k_rope, preferred_element_type=jnp.float32)) * scale
        s = jnp.where(kchunk[None, :] <= qchunk[:, None], s, NEG_INF)
        prob = jax.nn.softmax(s, axis=-1)
        return jnp.einsum('bhqk,bkhd->bqhd', prob.astype(v.dtype), v)

    out = lax.map(one_block, (jnp.arange(s_len // Q_BLOCK), to_blocks(q_nope), to_blocks(q_rope)))
    return from_blocks(out)


def mixer_a(h, w_in, w_out):
    b, s, _ = h.shape
    q, k, v, g = jnp.split(h @ w_in, 4, axis=-1)
    heads = lambda t: t.reshape(b, s, SB_HEADS, SB_HEAD_DIM)
    o = stick_breaking_attention(heads(q), heads(k), heads(v)).reshape(b, s, SB_WIDTH)
    return (o * jax.nn.silu(g)) @ w_out


def shared_kv(x, kv_norm, w_dkv, kv_latent_norm, w_uk, w_uv, cos, sin):
    b, s, _ = x.shape
    ckv, k_rope = jnp.split(rms_norm(x, kv_norm) @ w_dkv, [KV_LORA_RANK], axis=-1)
    ckv = rms_norm(ckv, kv_latent_norm)
    k_nope = (ckv @ w_uk).reshape(b, s, MLA_HEADS, QK_NOPE_DIM)
    v = (ckv @ w_uv).reshape(b, s, MLA_HEADS, V_HEAD_DIM)
    k_rope = apply_rope(k_rope, cos, sin)
    return k_nope, k_rope, v


def mixer_b(h, kv, w_in, q_latent_norm, w_uq, w_out, cos, sin):
    b, s, _ = h.shape
    k_nope, k_rope, v = kv
    cq, g = jnp.split(h @ w_in, [Q_LORA_RANK], axis=-1)
    q = (rms_norm(cq, q_latent_norm) @ w_uq).reshape(b, s, MLA_HEADS, QK_HEAD_DIM)
    q_nope, q_rope = jnp.split(q, [QK_NOPE_DIM], axis=-1)
    q_rope = apply_rope(q_rope, cos[:, None, :], sin[:, None, :])
    o = mla_attention(q_nope, q_rope, k_nope, k_rope, v).reshape(b, s, MLA_WIDTH)
    return (o * jax.nn.silu(g)) @ w_out


def residual_update(x, out, g_post, p_i, w_pp, w_pg):
    x = x + rms_norm(out, g_post)
    return x + (p_i @ w_pp) * jax.nn.sigmoid(x @ w_pg)


def setup_inputs(seed: int = 0) -> dict:
    key = jax.random.key(seed)
    ks = jax.random.split(key, 17)

    def w(k, shape, fan_in):
        return jax.random.normal(k, shape, jnp.float32) * fan_in ** -0.5

    def gain(k, shape):
        return 1.0 + 0.05 * jax.random.normal(k, shape, jnp.float32)

    return {
        "x": jax.random.normal(ks[0], (BATCH, SEQ, D_MODEL), jnp.float32),
        "p": jax.random.normal(ks[1], (DEPTH, BATCH, SEQ, PLE_DIM), jnp.float32),
        "norm_pre": gain(ks[2], (DEPTH, D_MODEL)),
        "norm_post": gain(ks[3], (DEPTH, D_MODEL)),
        "w_in_a": w(ks[4], (N_A, D_MODEL, 4 * SB_WIDTH), D_MODEL),
        "w_out_a": w(ks[5], (N_A, SB_WIDTH, D_MODEL), SB_WIDTH),
        "w_in_b": w(ks[6], (N_B, D_MODEL, Q_LORA_RANK + MLA_WIDTH), D_MODEL),
        "q_latent_norm": gain(ks[7], (N_B, Q_LORA_RANK)),
        "w_uq": w(ks[8], (N_B, Q_LORA_RANK, MLA_HEADS * QK_HEAD_DIM), Q_LORA_RANK),
        "w_out_b": w(ks[9], (N_B, MLA_WIDTH, D_MODEL), MLA_WIDTH),
        "kv_norm": gain(ks[10], (D_MODEL,)),
        "w_dkv": w(ks[11], (D_MODEL, KV_LORA_RANK + QK_ROPE_DIM), D_MODEL),
        "kv_latent_norm": gain(ks[12], (KV_LORA_RANK,)),
        "w_uk": w(ks[13], (KV_LORA_RANK, MLA_HEADS * QK_NOPE_DIM), KV_LORA_RANK),
        "w_uv": w(ks[14], (KV_LORA_RANK, MLA_WIDTH), KV_LORA_RANK),
        "w_ple_proj": w(ks[15], (DEPTH, PLE_DIM, D_MODEL), PLE_DIM),
        "w_ple_gate": w(ks[16], (DEPTH, D_MODEL, D_MODEL), D_MODEL),
    }


def reference(x, p, norm_pre, norm_post, w_in_a, w_out_a, w_in_b, q_latent_norm, w_uq, w_out_b,
              kv_norm, w_dkv, kv_latent_norm, w_uk, w_uv, w_ple_proj, w_ple_gate):
    cos, sin = rope_tables(x.shape[1])
    for i in range(N_A):
        out = mixer_a(rms_norm(x, norm_pre[i]), w_in_a[i], w_out_a[i])
        x = residual_update(x, out, norm_post[i], p[i], w_ple_proj[i], w_ple_gate[i])
    kv = shared_kv(x, kv_norm, w_dkv, kv_latent_norm, w_uk, w_uv, cos, sin)
    for j in range(N_B):
        i = N_A + j
        out = mixer_b(rms_norm(x, norm_pre[i]), kv, w_in_b[j], q_latent_norm[j], w_uq[j], w_out_b[j], cos, sin)
        x = residual_update(x, out, norm_post[i], p[i], w_ple_proj[i], w_ple_gate[i])
    return x
```

```python
import os
import numpy as np
from contextlib import ExitStack
import concourse.bass as bass
import concourse.mybir as mybir
from concourse.bass_utils import run_bass_kernel_spmd

F32 = mybir.dt.float32
BF16 = mybir.dt.bfloat16
AF = mybir.ActivationFunctionType
ALU = mybir.AluOpType
AX = mybir.AxisListType

D = 2048
SEQ = 4096
T = 1024
NB = 8
KC = 16
H = 16
EPS = 1e-6
NEG = -30000.0
GROUPS = [[0, 1, 2, 3], [4, 5, 6, 7]]
SC_A = 128 ** -0.5
SC_B = 192 ** -0.5


def qblock(r, s):
    return 8 * (s // 2) + (r if s % 2 == 0 else 7 - r)


class Dep:
    __slots__ = ("w", "r")

    def __init__(self):
        self.w = None
        self.r = {}


class SemObj:
    def __init__(self, h):
        self.h = h
        self.cnt = 0
        self.deps = []


class Sched:
    ENG = ("pe", "act", "dve", "pool", "sp")

    def __init__(self, nc, block):
        self.nc = nc
        self.block = block
        self.prog = {e: SemObj(nc.alloc_semaphore("prog_" + e)) for e in self.ENG}
        self.q = {e: [] for e in self.ENG}
        self.seen = {e: {} for e in self.ENG}
        self.dsems = []
        self.rr = 0
        self.ninst = 0
        self.stop_at = int(os.environ.get("KSTOP", "1000000000"))
        self.trace = False
        self.names = []

    def dsem(self, name):
        s = SemObj(self.nc.alloc_semaphore(name))
        self.dsems.append(s)
        return s

    def _wait(self, eng, sem, val):
        if val <= 0:
            return
        seen = self.seen[eng]
        if seen.get(sem, 0) >= val:
            return
        seen[sem] = val
        h = sem.h
        self.q[eng].append(lambda E, h=h, val=val: E.wait_ge(h, val))

    def op(self, eng, fn, R=(), W=(), dma=None, inc=None):
        own = self.prog[eng]
        if self.ninst >= self.stop_at:
            return
        self.ninst += 1
        if self.trace:
            self.names.append((self.ninst, eng))
        for d in R:
            if d.w is not None:
                self._wait(eng, d.w[0], d.w[1])
        strict = eng != "pe"
        for d in W:
            if d.w is not None and (strict or d.w[0] is not own):
                self._wait(eng, d.w[0], d.w[1])
            for sem, val in d.r.items():
                if strict or sem is not own:
                    self._wait(eng, sem, val)
        if dma is not None:
            dma.cnt += 16
            tok = (dma, dma.cnt)
            h = dma.h
            self.q[eng].append(lambda E, fn=fn, h=h: fn(E).then_inc(h, 16))
        elif inc is not None:
            inc.cnt += 1
            tok = (inc, inc.cnt)
            h = inc.h
            self.q[eng].append(lambda E, fn=fn, h=h: fn(E).then_inc(h, 1))
        else:
            own.cnt += 1
            tok = (own, own.cnt)
            h = own.h
            self.q[eng].append(lambda E, fn=fn, h=h: fn(E).then_inc(h, 1))
        for d in R:
            if d.r.get(tok[0], 0) < tok[1]:
                d.r[tok[0]] = tok[1]
        for d in W:
            d.w = tok
            d.r = {}
        if dma is not None:
            keep = []
            for d in dma.deps:
                if d.w is not None and d.w[0] is dma:
                    d.w = tok
                    keep.append(d)
            for d in W:
                if d not in keep:
                    keep.append(d)
            dma.deps = keep

    def wait_for_write(self, eng, W):
        for d in W:
            if d.w is not None:
                self._wait(eng, d.w[0], d.w[1])
            for sem, val in d.r.items():
                self._wait(eng, sem, val)

    def set_writer(self, W, sem):
        tok = (sem, sem.cnt)
        for d in W:
            d.w = tok
            d.r = {}
            if d not in sem.deps:
                sem.deps.append(d)

    def barrier(self):
        for e in self.ENG:
            for e2 in self.ENG:
                self._wait(e, self.prog[e2], self.prog[e2].cnt)
            for s in self.dsems:
                self._wait(e, s, s.cnt)

    def flush(self):
        m = {"pe": self.block.tensor, "act": self.block.scalar, "dve": self.block.vector,
             "pool": self.block.gpsimd, "sp": self.block.sync}
        for e in self.ENG:
            lst = self.q[e]
            if not lst:
                continue
            self.q[e] = []

            def body(E, lst=lst):
                for f in lst:
                    f(E)
            m[e](body)

    def end_phase(self):
        self.barrier()
        self.flush()

    def alt(self):
        self.rr ^= 1
        return "act" if self.rr else "dve"


def build_program(upto=99, debug=False):
    nc = bass.Bass("TRN2", target_bir_lowering=False)

    def din(name, shape, dt=F32):
        return nc.dram_tensor(name, list(shape), dt, kind="ExternalInput").ap()

    def dint(name, shape, dt=BF16):
        return nc.dram_tensor(name, list(shape), dt, kind="Internal").ap()

    x_own = din("x_own", [T, D])
    pT_in = din("pT", [2, 256, T])
    norm_pre = din("norm_pre", [2, D])
    norm_post = din("norm_post", [2, D])
    w_in_a = din("w_in_a", [D, 4 * D])
    w_out_a = din("w_out_a", [D, D])
    w_in_b = din("w_in_b", [D, 2560])
    q_lat_norm = din("q_latent_norm", [512])
    w_uq = din("w_uq", [512, 3072])
    w_out_b = din("w_out_b", [D, D])
    kv_norm = din("kv_norm", [D])
    w_dkv = din("w_dkv", [D, 576])
    kv_lat_norm = din("kv_latent_norm", [512])
    w_uk = din("w_uk", [512, D])
    w_uv = din("w_uv", [512, D])
    w_pp = din("w_ple_proj", [2, 256, D])
    w_pg = din("w_ple_gate", [2, D, D])
    m01_in = din("m01", [2, 128, 512])
    mneg_in = din("mneg", [2, 128, 512])
    mb1_in = din("mb1", [2, 128, 512])
    cosT_in = din("cosT", [64, T])
    sinT_in = din("sinT", [64, T])
    ident_in = din("ident", [128, 128])
    out_own = nc.dram_tensor("out_own", [T, D], F32, kind="ExternalOutput").ap()
    dbg = nc.dram_tensor("dbg", [T, D], F32, kind="ExternalOutput").ap() if debug else None

    kin = [[dint("kin%d_%d" % (l, g), [512, T]) for g in range(4)] for l in range(2)]
    kall = [[dint("kall%d_%d" % (l, g), [4 * 512, T]) for g in range(4)] for l in range(2)]
    vin = [[dint("vin%d_%d" % (l, g), [T, 512]) for g in range(4)] for l in range(2)]
    vall = [[dint("vall%d_%d" % (l, g), [4 * T, 512]) for g in range(4)] for l in range(2)]
    krin = dint("krin", [64, T])
    krall = dint("krall", [4 * 64, T])
    xs = dint("xs", [T, D], F32)

    A = nc.alloc_sbuf_tensor("bufA", [128, KC, T], BF16)
    B = nc.alloc_sbuf_tensor("bufB", [128, KC, T], BF16)
    A_d, B_d = Dep(), Dep()
    WN = 8192
    wbuf = [nc.alloc_sbuf_tensor("wbuf%d" % i, [128, WN], BF16) for i in range(2)]
    identb = nc.alloc_sbuf_tensor("identb", [128, 128], BF16)
    identf = nc.alloc_sbuf_tensor("identf", [128, 128], F32)
    onesf = nc.alloc_sbuf_tensor("onesf", [128, 128], F32)
    stats = nc.alloc_sbuf_tensor("stats", [128, 512], F32)
    cqT = nc.alloc_sbuf_tensor("cqT", [128, 4, T], BF16)
    ps = [nc.alloc_psum_tensor("ps%d" % i, [128, 512], F32) for i in range(6)]
    psb = [nc.alloc_psum_tensor("psb%d" % i, [128, 1024], BF16) for i in range(2)]

    with nc.Block() as block:
        sc = Sched(nc, block)
        op = sc.op
        ps_d = [Dep() for _ in ps]
        psb_d = [Dep() for _ in psb]
        wbuf_d = [Dep() for _ in wbuf]
        wsem = [sc.dsem("wsem%d" % i) for i in range(2)]
        ld_sem = [sc.dsem("ld%d" % i) for i in range(4)]
        st_sem = [sc.dsem("st%d" % i) for i in range(4)]
        misc_sem = sc.dsem("misc")
        pm_sem = sc.dsem("pmisc")
        kr_sem = sc.dsem("krld")
        whw_sem = sc.dsem("whw")
        cc_sem = SemObj(nc.alloc_semaphore("ccsem"))
        const_d = Dep()
        state = {"w": 0, "ps": 0, "psb": 0, "stat": 0}

        def next_ps(lo=0, hi=6):
            key = ("ps", lo, hi)
            i = state.get(key, lo)
            state[key] = lo + (i + 1 - lo) % (hi - lo)
            return ps[i], ps_d[i]

        def next_psb():
            i = state["psb"]
            state["psb"] = (i + 1) % len(psb)
            return psb[i], psb_d[i]

        stat_deps = [Dep() for _ in range(512)]

        def stat_col():
            i = state["stat"]
            state["stat"] = (i + 1) % 512
            return stats[:, i:i + 1], stat_deps[i]

        prefetched = {}

        def prefetch_w(key, src2d, kc, n):
            prefetched[key] = load_w(src2d, kc, n)

        def load_w(src2d, kc, n, key=None):
            if key is not None and key in prefetched:
                return prefetched.pop(key)
            i = state["w"]
            state["w"] = 1 - i
            dst = wbuf[i][:, 0:kc * n].rearrange("p (c n) -> p c n", n=n)
            src = src2d.rearrange("(c p) n -> p c n", p=128)
            stg = state.get("stg")
            if stg is not None and i == 1 and kc * n == WN:
                sg, sg_d = stg
                op("sp", lambda E: E.dma_start(out=sg[:, 0:kc * n].rearrange("p (c n) -> p c n", n=n), in_=src),
                   W=[sg_d], dma=whw_sem)
                hf = kc * n // 2
                op("act", lambda E: E.activation(out=wbuf[i][:, 0:hf], in_=sg[:, 0:hf], func=AF.Copy),
                   R=[sg_d], W=[wbuf_d[i]])
                op("dve", lambda E: E.tensor_copy(out=wbuf[i][:, hf:2 * hf], in_=sg[:, hf:2 * hf]),
                   R=[sg_d], W=[wbuf_d[i]])
                return dst, wbuf_d[i]
            op("pool", lambda E: E.dma_start(out=dst, in_=src), W=[wbuf_d[i]], dma=wsem[i])
            return dst, wbuf_d[i]

        def evac(out, in_, R, W, eng=None):
            e = eng or sc.alt()
            if e == "act":
                op(e, lambda E: E.activation(out=out, in_=in_, func=AF.Copy), R=R, W=W)
            else:
                op(e, lambda E: E.tensor_copy(out=out, in_=in_), R=R, W=W)

        def load_bcast(dst, vec_ap, d):
            op("sp", lambda E: E.dma_start(out=dst[:], in_=vec_ap.partition_broadcast(128)),
               W=[d], dma=misc_sem)

        def rstd_from_ssq(ssq, ssq_d, n):
            a, a_d = stat_col()
            b, b_d = stat_col()
            c, c_d = stat_col()
            op("dve", lambda E: E.tensor_scalar(out=a, in0=ssq, scalar1=1.0 / n, scalar2=EPS,
                                                op0=ALU.mult, op1=ALU.add), R=[ssq_d], W=[a_d])
            op("act", lambda E: E.activation(out=b, in_=a, func=AF.Sqrt), R=[a_d], W=[b_d])
            op("dve", lambda E: E.reciprocal(out=c, in_=b), R=[b_d], W=[c_d])
            return c, c_d

        def transpose_block(src, src_d, ncols, dst, dst_d, b):
            nch = ncols // 128
            for g0 in range(0, nch, 4):
                gn = min(4, nch - g0)
                pt, pt_d = next_psb()
                for j in range(gn):
                    c = g0 + j
                    op("pe", lambda E, c=c, j=j, pt=pt: E.transpose(
                        out=pt[:, j * 128:(j + 1) * 128], in_=src[:, c * 128:(c + 1) * 128],
                        identity=identb[:]), R=[src_d, const_d], W=[pt_d])
                o = dst[:, g0:g0 + gn, b * 128:(b + 1) * 128]
                i_ = pt[:, 0:gn * 128].rearrange("p (c n) -> p c n", n=128)
                evac(o, i_, [pt_d], [dst_d])

        def norm_transpose(get_x, gains, outs, junk, junk_d, xn, xn_d):
            for b in range(NB):
                xa, xa_d = get_x(b)
                ssq, ssq_d = stat_col()
                op("act", lambda E, xa=xa, ssq=ssq: E.activation(out=junk[:], in_=xa, func=AF.Square,
                                                                 accum_out=ssq),
                   R=[xa_d], W=[junk_d, ssq_d])
                r, r_d = rstd_from_ssq(ssq, ssq_d, D)
                op("act", lambda E, xa=xa, r=r: E.activation(out=xa, in_=xa, func=AF.Copy, scale=r),
                   R=[xa_d, r_d], W=[xa_d])
                for gi, ((gb, gb_d), (dst, dst_d)) in enumerate(zip(gains, outs)):
                    k = (b * len(gains) + gi) % 2
                    eng = "dve" if gi == 0 else "pool"
                    op(eng, lambda E, xa=xa, gb=gb, k=k: E.tensor_tensor(
                        out=xn[k][:], in0=xa, in1=gb[:], op=ALU.mult),
                       R=[xa_d, gb_d], W=[xn_d[k]])
                    transpose_block(xn[k], xn_d[k], D, dst, dst_d, b)

        deferred = {0: {}, 1: {}}

        def gather(src, dst, src_ds, dst_d, defer=None):
            if defer is not None:
                deferred[defer[0]].setdefault(defer[1], []).append((src, dst, list(src_ds), dst_d))
                return
            op("pool", lambda E: E.collective_compute("AllGather", ALU.bypass, replica_groups=GROUPS,
                                                      ins=[src], outs=[dst]), R=src_ds, W=[dst_d], inc=cc_sem)

        def emit_deferred(l, cg, which):
            lst = deferred[l].get(cg, [])
            if which < len(lst) and lst[which] is not None:
                src, dst, src_ds, dst_d = lst[which]
                lst[which] = None
                gather(src, dst, src_ds, dst_d)

        def proj_fm(wt, wd, kc, col0, m, src, src_d, dst, dst_d, func=None):
            for half in range(2):
                p, p_d = next_ps()
                for c in range(kc):
                    op("pe", lambda E, c=c, p=p, half=half: E.matmul(
                        p[0:m, :], lhsT=wt[:, c, col0:col0 + m], rhs=src[:, c, half * 512:(half + 1) * 512],
                        start=(c == 0), stop=(c == kc - 1)), R=[wd, src_d], W=[p_d])
                o = dst[0:m, half * 512:(half + 1) * 512]
                if func is None:
                    evac(o, p[0:m, :], [p_d], [dst_d])
                else:
                    op("act", lambda E, o=o, p=p: E.activation(out=o, in_=p[0:m, :], func=func),
                       R=[p_d], W=[dst_d])

        op("pool", lambda E: E.dma_start(out=identb[:], in_=ident_in), W=[const_d], dma=pm_sem)
        op("sp", lambda E: E.dma_start(out=identf[:], in_=ident_in), W=[const_d], dma=misc_sem)
        op("pool", lambda E: E.memset(onesf[:], 1.0), W=[const_d])
        sc.end_phase()

        kall_d = [[Dep() for _ in range(4)] for _ in range(2)]
        vall_d = [[Dep() for _ in range(4)] for _ in range(2)]
        krall_d = Dep()

        def P01():
          with ExitStack() as es:
            def al(name, shape, dt):
                return es.enter_context(nc.sbuf_tensor(name, shape, dt))
            xt = [al("xt%d" % i, [128, D], F32) for i in range(2)]
            junk = al("junk", [128, D], BF16)
            xn = [al("xn%d" % i, [128, D], BF16) for i in range(2)]
            gb0 = al("gb0", [128, D], F32)
            kst = [al("kst%d" % i, [128, T], BF16) for i in range(2)]
            vst = [al("vst%d" % i, [128, 512], BF16) for i in range(2)]
            stg01 = al("stg01", [128, WN], F32)
            state["stg"] = (stg01[:, :], Dep())
            xt_d = [Dep(), Dep()]
            gb0_d = Dep()
            load_bcast(gb0, norm_pre[0], gb0_d)

            def get_x0(b):
                k = b % 2
                op("sp", lambda E: E.dma_start(out=xt[k][:], in_=x_own[b * 128:(b + 1) * 128, :]),
                   W=[xt_d[k]], dma=ld_sem[k])
                return xt[k][:], xt_d[k]

            norm_transpose(get_x0, [(gb0, gb0_d)], [(A, A_d)], junk, Dep(), xn, [Dep(), Dep()])
            kst_d = [Dep(), Dep()]
            vst_d = [Dep(), Dep()]
            kin_d = [Dep() for _ in range(H)]
            vin_d = [Dep() for _ in range(32)]
            i = 0
            steps = []
            for cg in range(4):
                steps.append(("k", cg, w_in_a[:, D + cg * 512: D + (cg + 1) * 512]))
                steps.append(("v", cg, w_in_a[:, 2 * D + cg * 512: 2 * D + (cg + 1) * 512]))
            loaded = {0: load_w(steps[0][2], KC, 512)}
            for si, (kind, cg, _) in enumerate(steps):
                wt, wd = loaded.pop(si)
                if si + 1 < len(steps):
                    loaded[si + 1] = load_w(steps[si + 1][2], KC, 512)
                if kind == "k":
                    for hh in range(4):
                        h = cg * 4 + hh
                        k = h % 2
                        proj_fm(wt, wd, KC, hh * 128, 128, A, A_d, kst[k], kst_d[k])
                        op("sp", lambda E, hh=hh, cg=cg, k=k: E.dma_start(
                            out=kin[0][cg][hh * 128:(hh + 1) * 128, :], in_=kst[k][:]),
                           R=[kst_d[k]], W=[kin_d[h]], dma=st_sem[k])
                    gather(kin[0][cg], kall[0][cg], kin_d[cg * 4:cg * 4 + 4], kall_d[0][cg],
                           defer=(0, cg) if cg > 0 else None)
                else:
                    for b in range(NB):
                        k = i % 2
                        i += 1
                        p, p_d = next_ps()
                        for c in range(KC):
                            op("pe", lambda E, c=c, p=p, wt=wt, b=b: E.matmul(
                                p[:], lhsT=A[:, c, b * 128:(b + 1) * 128], rhs=wt[:, c, :],
                                start=(c == 0), stop=(c == KC - 1)), R=[wd, A_d], W=[p_d])
                        evac(vst[k][:], p[:], [p_d], [vst_d[k]])
                        op("sp", lambda E, b=b, cg=cg, k=k: E.dma_start(
                            out=vin[0][cg][b * 128:(b + 1) * 128, :], in_=vst[k][:]),
                           R=[vst_d[k]], W=[vin_d[cg * NB + b]], dma=st_sem[2 + k])
                    gather(vin[0][cg], vall[0][cg], vin_d[cg * NB:(cg + 1) * NB], vall_d[0][cg],
                           defer=(0, cg) if cg > 0 else None)
            state["stg"] = None
            sc.end_phase()

        def load_kv_head(l, h, kh, kh_d, vh, vh_d, sem):
            sc.wait_for_write("sp", [kh_d, vh_d])
            for r in range(4):
                for par in range(2):
                    nb0 = r if par == 0 else 7 - r
                    src = kall[l][h // 4][(r * 4 + h % 4) * 128:(r * 4 + h % 4 + 1) * 128, :].rearrange(
                        "p (s two n) -> p two s n", two=2, n=128)[:, par, :, :]
                    dst = kh[:, :].rearrange("p (s e n) -> p e s n", e=8, n=128)[:, nb0, :, :]
                    op("sp", lambda E, src=src, dst=dst: E.dma_start(out=dst, in_=src),
                       R=[kall_d[l][h // 4]], W=[], dma=sem)
                    srcv = vall[l][h // 4][r * T:(r + 1) * T, (h % 4) * 128:(h % 4 + 1) * 128].rearrange(
                        "(s two p) n -> p two s n", two=2, p=128)[:, par, :, :]
                    dstv = vh[:, :, :].rearrange("p (s e) n -> p e s n", e=8)[:, nb0, :, :]
                    op("sp", lambda E, srcv=srcv, dstv=dstv: E.dma_start(out=dstv, in_=srcv),
                       R=[vall_d[l][h // 4]], W=[], dma=sem)
            sc.set_writer([kh_d, vh_d], sem)

        def pv_T(wbt_k, wb_dk, wTt_k, wT_dk):
            pt, pt_d = next_psb()
            for j in range(4):
                op("pe", lambda E, j=j, pt=pt: E.transpose(
                    out=pt[:, j * 128:(j + 1) * 128], in_=wbt_k[:, j * 128:(j + 1) * 128],
                    identity=identb[:]), R=[wb_dk, const_d], W=[pt_d])
            evac(wTt_k[:], pt[:, 0:512], [pt_d], [wT_dk], eng="dve")

        def pv_mm(wTt_k, wT_dk, vh_h, vh_dh, po, po_d, c, nt):
            for j in range(4):
                kb = c * 4 + j
                op("pe", lambda E, j=j, kb=kb: E.matmul(
                    po[:, 0:128], lhsT=vh_h[:, kb, :], rhs=wTt_k[:, j * 128:(j + 1) * 128],
                    start=(kb == 0), stop=(kb == 4 * nt - 1)),
                   R=[vh_dh, wT_dk], W=[po_d])

        def P3():
          with ExitStack() as es:
            def al(name, shape, dt):
                return es.enter_context(nc.sbuf_tensor(name, shape, dt))
            kh = [al("kh%d" % i, [128, SEQ], BF16) for i in range(2)]
            vh = [al("vh%d" % i, [128, 32, 128], BF16) for i in range(2)]
            qT = [al("qT%d" % i, [128, T], BF16) for i in range(2)]
            gT = [al("gT%d" % i, [128, T], BF16) for i in range(2)]
            eb = al("eb", [128, 9, 512], F32)
            Pb = al("Pb", [128, 9 * 513 + 8], F32)
            spt = [al("spt%d" % i, [128, 520], F32) for i in range(2)]
            ut = [al("ut%d" % i, [128, 512], F32) for i in range(2)]
            wbt = [al("wb%d" % i, [128, 512], BF16) for i in range(3)]
            wTt = [al("wT%d" % i, [128, 512], BF16) for i in range(3)]
            m01 = al("m01s", [128, 2, 512], F32)
            ones513 = al("ones513", [128, 520], F32)
            kh_d, vh_d, qT_d, gT_d = ([Dep(), Dep()] for _ in range(4))
            spt_d, ut_d, wb_d, wT_d = ([Dep(), Dep(), Dep()] for _ in range(4))
            e_d = [Dep() for _ in range(9)]
            P_d = [Dep() for _ in range(10)]
            mask_d = Dep()
            op("sp", lambda E: E.dma_start(out=m01[:], in_=m01_in.rearrange("k p n -> p k n")),
               W=[mask_d], dma=misc_sem)
            op("pool", lambda E: E.memset(ones513[:], 1.0), W=[const_d])
            for k in range(2):
                op("pool", lambda E, k=k: E.memset(spt[k][:, 512:520], 0.0), W=[spt_d[k]])
            cnt = {"a": 0, "b": 0}

            hw = {}

            def head_load(h):
                hb = h % 2
                load_kv_head(0, h, kh[hb], kh_d[hb], vh[hb], vh_d[hb], ld_sem[hb])
                hw[h] = (load_w(w_in_a[:, h * 128:(h + 1) * 128], KC, 128),
                         load_w(w_in_a[:, 3 * D + h * 128: 3 * D + (h + 1) * 128], KC, 128))

            def head_proj(h):
                hb = h % 2
                (wt, wd), (wt2, wd2) = hw.pop(h)
                proj_fm(wt, wd, KC, 0, 128, A, A_d, qT[hb], qT_d[hb])
                proj_fm(wt2, wd2, KC, 0, 128, A, A_d, gT[hb], gT_d[hb], func=AF.Silu)

            def ph1a(sl, c, r):
                h, s, nt = sl["h"], sl["s"], sl["nt"]
                hb = h % 2
                p, p_d = next_ps(0, 4)
                qs = qT[hb][:, s * 128:(s + 1) * 128]
                op("pe", lambda E: E.matmul(p[:], lhsT=qs, rhs=kh[hb][:, c * 512:(c + 1) * 512],
                                            start=True, stop=True), R=[qT_d[hb], kh_d[hb]], W=[p_d])
                op("act", lambda E: E.activation(out=eb[:, r, :], in_=p[:], func=AF.Exp, scale=SC_A),
                   R=[p_d], W=[e_d[r]])
                if c == nt - 1:
                    par = s % 2
                    op("pool", lambda E: E.tensor_tensor(out=eb[:, r, :], in0=eb[:, r, :], in1=m01[:, par, :],
                                                         op=ALU.mult), R=[e_d[r], mask_d], W=[e_d[r]])
                if c == 0:
                    op("pool", lambda E: E.memset(Pb[:, 513 * r:513 * r + 1], 0.0), W=[P_d[r]])

            def ph1b(sl, c, r):
                nt = sl["nt"]
                k = cnt["a"] % 2
                cnt["a"] += 1
                op("act", lambda E: E.activation(out=spt[k][:, 0:512], in_=eb[:, r, :], func=AF.Ln, bias=1.0),
                   R=[e_d[r]], W=[spt_d[k]])
                b0 = 513 * r
                n = 512 if c == nt - 1 else 513
                init = 0.0 if c == 0 else Pb[:, b0:b0 + 1]
                wr = [P_d[r]] + ([P_d[r + 1]] if n == 513 else [])
                op("dve", lambda E: E.tensor_tensor_scan(
                    out=Pb[:, b0 + 1:b0 + 1 + n], data0=ones513[:, 0:n], data1=spt[k][:, 0:n],
                    initial=init, op0=ALU.mult, op1=ALU.add),
                   R=[spt_d[k], const_d, P_d[r]], W=wr)

            def mid(sl):
                rl = sl["r0"] + sl["nt"] - 1
                nP, nP_d = stat_col()
                op("dve", lambda E: E.tensor_scalar(out=nP, in0=Pb[:, 513 * rl + 512:513 * rl + 513], scalar1=-1.0,
                                                    scalar2=None, op0=ALU.mult), R=[P_d[rl]], W=[nP_d])
                sl["nP"] = (nP, nP_d)
                sl["po"] = next_ps(4, 6)

            def stC_act(sl, c, r, i):
                k = i % 2
                nP, nP_d = sl["nP"]
                op("act", lambda E: E.activation(out=ut[k][:], in_=Pb[:, 513 * r:513 * r + 512], func=AF.Exp,
                                                 bias=nP, scale=1.0), R=[P_d[r], nP_d], W=[ut_d[k]])

            def stC_pool(sl, c, r, i):
                k = i % 2
                k3 = i % 3
                op("pool", lambda E: E.tensor_tensor(out=wbt[k3][:], in0=eb[:, r, :], in1=ut[k][:], op=ALU.mult),
                   R=[e_d[r], ut_d[k]], W=[wb_d[k3]])

            def stD(sl, c, r, i):
                k3 = i % 3
                pv_T(wbt[k3], wb_d[k3], wTt[k3], wT_d[k3])

            def stE(sl, c, r, i):
                h, s, nt = sl["h"], sl["s"], sl["nt"]
                hb = h % 2
                k3 = i % 3
                po, po_d = sl["po"]
                pv_mm(wTt[k3], wT_d[k3], vh[hb], vh_d[hb], po, po_d, c, nt)
                if c == nt - 1:
                    op("dve", lambda E: E.tensor_tensor(
                        out=B[:, h, s * 128:(s + 1) * 128], in0=po[:, 0:128],
                        in1=gT[hb][:, s * 128:(s + 1) * 128], op=ALU.mult), R=[po_d, gT_d[hb]], W=[B_d])

            def run_round(cur, prev):
                ct = cur["tiles"] if cur else []
                pt_ = prev["tiles"] if prev else []
                if prev is not None:
                    for sl in prev["slots"]:
                        mid(sl)
                n = max(len(ct), len(pt_))
                for t in range(-1, n + 2):
                    if 0 <= t + 1 < len(pt_):
                        stC_act(*pt_[t + 1])
                    if 0 <= t < len(ct):
                        ph1a(*ct[t])
                    if 0 <= t + 1 < len(pt_):
                        stC_pool(*pt_[t + 1])
                    if 0 <= t < len(pt_):
                        stD(*pt_[t])
                    if 0 <= t - 1 < len(pt_):
                        stE(*pt_[t - 1])
                    if 0 <= t - 1 < len(ct):
                        ph1b(*ct[t - 1])

            def run_round_mixed(cur1, cur, prev):
                run_round(cur1, prev)

            prev = None
            head_load(0)
            head_proj(0)
            gi = 0
            for h in range(H):
                for ri, pair in enumerate(((0, 7), (1, 6), (2, 5), (3, 4))):
                    if h in (0, 2, 4) and ri == 2:
                        emit_deferred(0, 1 + h // 2, 0)
                    if h in (1, 3, 5) and ri == 0:
                        emit_deferred(0, 1 + h // 2, 1)
                    if ri == 1 and h + 1 < H:
                        head_load(h + 1)
                    if ri == 3 and h + 1 < H:
                        head_proj(h + 1)
                    if ri == 0 and h == H - 1:
                        prefetch_w(("out", 0, 0), w_out_a[:, 0:512], KC, 512)
                        prefetch_w(("out", 0, 1), w_out_a[:, 512:1024], KC, 512)
                    slots = []
                    r = 0
                    for s_ in pair:
                        slots.append({"h": h, "s": s_, "nt": s_ + 1, "r0": r})
                        r += s_ + 1
                    tiles = []
                    for sl in slots:
                        for c in range(sl["nt"]):
                            tiles.append((sl, c, sl["r0"] + c))
                    cur = {"slots": slots, "tiles": [(sl, c, r_, gi + i) for i, (sl, c, r_) in enumerate(tiles)]}
                    gi += len(tiles)
                    cur1 = {"slots": slots, "tiles": [(sl, c, r_) for (sl, c, r_, _) in cur["tiles"]]}
                    run_round_mixed(cur1, cur, prev)
                    prev = cur
            run_round_mixed(None, None, prev)
            sc.end_phase()

        def P_out(l, w_out, ybuf, y_d, x_src, es):
            def al(name, shape, dt):
                return es.enter_context(nc.sbuf_tensor(name, shape, dt))
            xt = [al("xo%d_%d" % (l, i), [128, D], F32) for i in range(2)]
            junk = al("junko%d" % l, [128, D], BF16)
            gp = al("gpost%d" % l, [128, D], F32)
            xt_d = [Dep(), Dep()]
            junk_d = Dep()
            gp_d = Dep()
            load_bcast(gp, norm_post[l], gp_d)
            state["stg"] = (A.bitcast(F32)[:, :, :].rearrange("p c n -> p (c n)"), A_d)
            for n in range(4):
                wt, wd = load_w(w_out[:, n * 512:(n + 1) * 512], KC, 512, key=("out", l, n))
                for b in range(NB):
                    p, p_d = next_ps()
                    for c in range(KC):
                        op("pe", lambda E, c=c, p=p, wt=wt, b=b: E.matmul(
                            p[:], lhsT=B[:, c, b * 128:(b + 1) * 128], rhs=wt[:, c, :],
                            start=(c == 0), stop=(c == KC - 1)), R=[wd, B_d], W=[p_d])
                    evac(ybuf[:, b, n * 512:(n + 1) * 512], p[:], [p_d], [y_d[b]])
            prefetch_w(("pg", l, 0), w_pg[l][:, 0:512], KC, 512)
            state["stg"] = None
            for b in range(NB):
                k = b % 2
                op("sp", lambda E, b=b, k=k: E.dma_start(out=xt[k][:], in_=x_src[b * 128:(b + 1) * 128, :]),
                   W=[xt_d[k]], dma=ld_sem[k])
                ssq, ssq_d = stat_col()
                op("act", lambda E, b=b, ssq=ssq: E.activation(out=junk[:], in_=ybuf[:, b, :], func=AF.Square,
                                                               accum_out=ssq), R=[y_d[b]], W=[junk_d, ssq_d])
                r, r_d = rstd_from_ssq(ssq, ssq_d, D)
                op("dve", lambda E, b=b, r=r: E.scalar_tensor_tensor(
                    out=ybuf[:, b, :], in0=ybuf[:, b, :], scalar=r, in1=gp[:], op0=ALU.mult, op1=ALU.mult),
                   R=[y_d[b], r_d, gp_d], W=[y_d[b]])
                op("pool", lambda E, b=b, k=k: E.tensor_tensor(
                    out=ybuf[:, b, :], in0=ybuf[:, b, :], in1=xt[k][:], op=ALU.add),
                   R=[y_d[b], xt_d[k]], W=[y_d[b]])

        def P_ple(l, ybuf, y_d, es):
            def al(name, shape, dt):
                return es.enter_context(nc.sbuf_tensor(name, shape, dt))
            xb16 = [al("xb16_%d_%d" % (l, i), [128, D], BF16) for i in range(2)]
            sgt = [al("sgt%d_%d" % (l, i), [128, 512], F32) for i in range(2)]
            tt = [al("tt%d_%d" % (l, i), [128, 512], F32) for i in range(2)]
            pTt = al("pTt%d" % l, [128, 2, T], BF16)
            wppt = [al("wppt%d_%d" % (l, i), [128, 2, 512], BF16) for i in range(2)]
            xb_d = [Dep(), Dep()]
            sg_d = [Dep(), Dep()]
            tt_d = [Dep(), Dep()]
            pT_d = Dep()
            wpp_d = [Dep(), Dep()]
            op("pool", lambda E: E.dma_start(out=pTt[:], in_=pT_in[l].rearrange("(j p) t -> p j t", p=128)),
               W=[pT_d], dma=pm_sem)
            for b in range(NB):
                k = b % 2
                op("pool", lambda E, b=b, k=k: E.tensor_copy(out=xb16[k][:], in_=ybuf[:, b, :]),
                   R=[y_d[b]], W=[xb_d[k]])
                transpose_block(xb16[k], xb_d[k], D, A, A_d, b)
            i = 0
            state["stg"] = (B.bitcast(F32)[:, :, :].rearrange("p c n -> p (c n)"), B_d)
            for n in range(4):
                wt, wd = load_w(w_pg[l][:, n * 512:(n + 1) * 512], KC, 512, key=("pg", l, n))
                kk = n % 2
                op("pool", lambda E, n=n, kk=kk: E.dma_start(
                    out=wppt[kk][:], in_=w_pp[l][:, n * 512:(n + 1) * 512].rearrange("(j p) n -> p j n", p=128)),
                   W=[wpp_d[kk]], dma=ld_sem[2 + kk])
                for b in range(NB):
                    k = i % 2
                    i += 1
                    pa, pa_d = next_ps()
                    for c in range(KC):
                        op("pe", lambda E, c=c, pa=pa, wt=wt, b=b: E.matmul(
                            pa[:], lhsT=A[:, c, b * 128:(b + 1) * 128], rhs=wt[:, c, :],
                            start=(c == 0), stop=(c == KC - 1)), R=[wd, A_d], W=[pa_d])
                    pb, pb_d = next_ps()
                    for j in range(2):
                        op("pe", lambda E, j=j, pb=pb, b=b, kk=kk: E.matmul(
                            pb[:], lhsT=pTt[:, j, b * 128:(b + 1) * 128], rhs=wppt[kk][:, j, :],
                            start=(j == 0), stop=(j == 1)), R=[pT_d, wpp_d[kk]], W=[pb_d])
                    op("act", lambda E, pa=pa, k=k: E.activation(out=sgt[k][:], in_=pa[:], func=AF.Sigmoid),
                       R=[pa_d], W=[sg_d[k]])
                    op("dve", lambda E, pb=pb, k=k: E.tensor_tensor(out=tt[k][:], in0=pb[:], in1=sgt[k][:],
                                                                    op=ALU.mult), R=[pb_d, sg_d[k]], W=[tt_d[k]])
                    op("pool", lambda E, b=b, n=n, k=k: E.tensor_tensor(
                        out=ybuf[:, b, n * 512:(n + 1) * 512], in0=ybuf[:, b, n * 512:(n + 1) * 512],
                        in1=tt[k][:], op=ALU.add), R=[y_d[b], tt_d[k]], W=[y_d[b]])

        def store_rows(dst, ybuf, y_d, sem, dst_d=None):
            state["stg"] = None
            for b in range(NB):
                op("sp", lambda E, b=b: E.dma_start(out=dst[b * 128:(b + 1) * 128, :], in_=ybuf[:, b, :]),
                   R=[y_d[b]], W=[dst_d] if dst_d is not None else [], dma=sem)

        def latent_norm_T(es, tag, wsrc, src, src_d, gvec, dst, dst_d):
            def al(name, shape, dt):
                return es.enter_context(nc.sbuf_tensor(name, shape, dt))
            gl = al("gl" + tag, [128, 512], F32)
            cn = [al("cn%s%d" % (tag, i), [128, 512], BF16) for i in range(2)]
            jk = al("jk" + tag, [128, 512], BF16)
            gl_d, jk_d = Dep(), Dep()
            cn_d = [Dep(), Dep()]
            load_bcast(gl, gvec, gl_d)
            wt, wd = load_w(wsrc, KC, 512, key=("lat", tag))
            for b in range(NB):
                k = b % 2
                p, p_d = next_ps()
                for c in range(KC):
                    op("pe", lambda E, c=c, p=p, b=b: E.matmul(
                        p[:], lhsT=src[:, c, b * 128:(b + 1) * 128], rhs=wt[:, c, :],
                        start=(c == 0), stop=(c == KC - 1)), R=[wd, src_d], W=[p_d])
                ssq, ssq_d = stat_col()
                op("act", lambda E, p=p, ssq=ssq: E.activation(out=jk[:], in_=p[:], func=AF.Square, accum_out=ssq),
                   R=[p_d], W=[jk_d, ssq_d])
                r, r_d = rstd_from_ssq(ssq, ssq_d, 512)
                op("dve", lambda E, p=p, r=r, k=k: E.scalar_tensor_tensor(
                    out=cn[k][:], in0=p[:], scalar=r, in1=gl[:], op0=ALU.mult, op1=ALU.mult),
                   R=[p_d, r_d, gl_d], W=[cn_d[k]])
                transpose_block(cn[k], cn_d[k], 512, dst, dst_d, b)

        def rope_fm(wt, wd, kc, colA, wsw, wsw_d, src, src_d, cosT, sinT, cs_d, t1, t1_d, t2, t2_d, dst, dst_d):
            for half in range(2):
                pa, pa_d = next_ps()
                pb, pb_d = next_ps()
                for c in range(kc):
                    op("pe", lambda E, c=c, pa=pa, half=half: E.matmul(
                        pa[0:64, :], lhsT=wt[:, c, colA:colA + 64], rhs=src[:, c, half * 512:(half + 1) * 512],
                        start=(c == 0), stop=(c == kc - 1)), R=[wd, src_d], W=[pa_d])
                for c in range(kc):
                    op("pe", lambda E, c=c, pb=pb, half=half: E.matmul(
                        pb[0:64, :], lhsT=wsw[:, c, 0:64], rhs=src[:, c, half * 512:(half + 1) * 512],
                        start=(c == 0), stop=(c == kc - 1)), R=[wsw_d, src_d], W=[pb_d])
                hs = slice(half * 512, (half + 1) * 512)
                op("dve", lambda E, pa=pa, hs=hs: E.tensor_tensor(out=t1[0:64, :], in0=pa[0:64, :],
                                                                  in1=cosT[0:64, hs], op=ALU.mult),
                   R=[pa_d, cs_d], W=[t1_d])
                op("dve", lambda E, pb=pb, hs=hs: E.tensor_tensor(out=t2[0:64, :], in0=pb[0:64, :],
                                                                  in1=sinT[0:64, hs], op=ALU.mult),
                   R=[pb_d, cs_d], W=[t2_d])
                op("pool", lambda E, hs=hs: E.tensor_tensor(out=dst[0:64, hs], in0=t1[0:64, :], in1=t2[0:64, :],
                                                            op=ALU.add), R=[t1_d, t2_d], W=[dst_d])

        def build_swap(wsw, wsw_d, wt, wd, kc, colA):
            op("pool", lambda E: E.tensor_copy(out=wsw[:, 0:kc, 0:32], in_=wt[:, :, colA + 32:colA + 64]),
               R=[wd], W=[wsw_d])
            op("pool", lambda E: E.tensor_copy(out=wsw[:, 0:kc, 32:64], in_=wt[:, :, colA:colA + 32]),
               R=[wd], W=[wsw_d])

        def P6a(ybuf, y_d, es):
            def al(name, shape, dt):
                return es.enter_context(nc.sbuf_tensor(name, shape, dt))
            junk = al("junk6", [128, D], BF16)
            xn = [al("xn6_%d" % i, [128, D], BF16) for i in range(2)]
            gkv = al("gkv", [128, D], F32)
            g1 = al("g1", [128, D], F32)
            gkv_d, g1_d = Dep(), Dep()
            load_bcast(gkv, kv_norm, gkv_d)
            load_bcast(g1, norm_pre[1], g1_d)

            def get_x(b):
                return ybuf[:, b, :], y_d[b]
            norm_transpose(get_x, [(gkv, gkv_d), (g1, g1_d)], [(B, B_d), (A, A_d)], junk, Dep(), xn, [Dep(), Dep()])

        def P6b(es):
            def al(name, shape, dt):
                return es.enter_context(nc.sbuf_tensor(name, shape, dt))
            ckvT = al("ckvT", [128, 4, T], BF16)
            kst = [al("kst6_%d" % i, [128, T], BF16) for i in range(2)]
            vst = [al("vst6_%d" % i, [128, 512], BF16) for i in range(2)]
            wr = al("wr6", [128, KC, 64], BF16)
            wsw = al("wsw6", [128, KC, 64], BF16)
            cosT = al("cos6", [64, T], F32)
            sinT = al("sin6", [64, T], F32)
            t1 = al("t1_6", [64, 512], F32)
            t2 = al("t2_6", [64, 512], F32)
            krT = al("krT6", [64, T], BF16)
            ckvT_d, wr_d, wsw_d, cs_d = Dep(), Dep(), Dep(), Dep()
            t1_d, t2_d, krT_d = Dep(), Dep(), Dep()
            op("sp", lambda E: E.dma_start(out=cosT[:], in_=cosT_in), W=[cs_d], dma=misc_sem)
            op("sp", lambda E: E.dma_start(out=sinT[:], in_=sinT_in), W=[cs_d], dma=misc_sem)
            op("pool", lambda E: E.dma_start(out=wr[:], in_=w_dkv[:, 512:576].rearrange("(c p) n -> p c n", p=128)),
               W=[wr_d], dma=ld_sem[2])
            build_swap(wsw, wsw_d, wr, wr_d, KC, 0)
            latent_norm_T(es, "kv", w_dkv[:, 0:512], B, B_d, kv_lat_norm, ckvT, ckvT_d)
            rope_fm(wr, wr_d, KC, 0, wsw, wsw_d, B, B_d, cosT, sinT, cs_d, t1, t1_d, t2, t2_d, krT, krT_d)
            kr_in_d = Dep()
            op("sp", lambda E: E.dma_start(out=krin, in_=krT[:]), R=[krT_d], W=[kr_in_d], dma=st_sem[0])
            gather(krin, krall, [kr_in_d], krall_d)
            kst_d = [Dep(), Dep()]
            vst_d = [Dep(), Dep()]
            kin_d = [Dep() for _ in range(H)]
            vin_d = [Dep() for _ in range(32)]
            i = 0
            steps = []
            for cg in range(4):
                steps.append(("k", cg, w_uk[:, cg * 512:(cg + 1) * 512]))
                steps.append(("v", cg, w_uv[:, cg * 512:(cg + 1) * 512]))
            loaded = {0: load_w(steps[0][2], 4, 512)}
            for si, (kind, cg, _) in enumerate(steps):
                wt, wd = loaded.pop(si)
                if si + 1 < len(steps):
                    loaded[si + 1] = load_w(steps[si + 1][2], 4, 512)
                if kind == "k":
                    for hh in range(4):
                        h = cg * 4 + hh
                        k = h % 2
                        proj_fm(wt, wd, 4, hh * 128, 128, ckvT, ckvT_d, kst[k], kst_d[k])
                        op("sp", lambda E, hh=hh, cg=cg, k=k: E.dma_start(
                            out=kin[1][cg][hh * 128:(hh + 1) * 128, :], in_=kst[k][:]),
                           R=[kst_d[k]], W=[kin_d[h]], dma=st_sem[k])
                    gather(kin[1][cg], kall[1][cg], kin_d[cg * 4:cg * 4 + 4], kall_d[1][cg],
                           defer=(1, cg) if cg > 0 else None)
                else:
                    for b in range(NB):
                        k = i % 2
                        i += 1
                        p, p_d = next_ps()
                        for c in range(4):
                            op("pe", lambda E, c=c, p=p, wt=wt, b=b: E.matmul(
                                p[:], lhsT=ckvT[:, c, b * 128:(b + 1) * 128], rhs=wt[:, c, :],
                                start=(c == 0), stop=(c == 3)), R=[wd, ckvT_d], W=[p_d])
                        evac(vst[k][:], p[:], [p_d], [vst_d[k]])
                        op("sp", lambda E, b=b, cg=cg, k=k: E.dma_start(
                            out=vin[1][cg][b * 128:(b + 1) * 128, :], in_=vst[k][:]),
                           R=[vst_d[k]], W=[vin_d[cg * NB + b]], dma=st_sem[2 + k])
                    gather(vin[1][cg], vall[1][cg], vin_d[cg * NB:(cg + 1) * NB], vall_d[1][cg],
                           defer=(1, cg) if cg > 0 else None)
            latent_norm_T(es, "q", w_in_b[:, 0:512], A, A_d, q_lat_norm, cqT, cqT_d)

        def P9():
          with ExitStack() as es:
            def al(name, shape, dt):
                return es.enter_context(nc.sbuf_tensor(name, shape, dt))
            kh = [al("kh9_%d" % i, [128, SEQ], BF16) for i in range(2)]
            vh = [al("vh9_%d" % i, [128, 32, 128], BF16) for i in range(2)]
            kr = al("kr9", [64, SEQ], BF16)
            scb = al("scb", [128, 9, 512], F32)
            qn = [al("qn%d" % i, [128, T], BF16) for i in range(2)]
            qr = [al("qr%d" % i, [64, T], BF16) for i in range(2)]
            gT = [al("gT9_%d" % i, [128, T], BF16) for i in range(2)]
            cosT = al("cos9", [64, T], F32)
            sinT = al("sin9", [64, T], F32)
            mb1 = al("mb1s", [128, 2, 512], F32)
            wbt = [al("wb9_%d" % i, [128, 512], BF16) for i in range(3)]
            wTt = [al("wT9_%d" % i, [128, 512], BF16) for i in range(3)]
            t1 = al("t1_9", [64, 512], F32)
            t2 = al("t2_9", [64, 512], F32)
            wsw = al("wsw9", [128, 4, 64], BF16)
            mxt = [al("mxt%d" % i, [128, 8], F32) for i in range(4)]
            rst = [al("rst%d" % i, [128, 8], F32) for i in range(4)]
            dg = [al("dg%d" % i, [128, 128], F32) for i in range(2)]
            tg = [al("tg%d" % i, [128, 128], F32) for i in range(2)]
            kh_d, vh_d, qn_d, qr_d, gT_d = ([Dep(), Dep()] for _ in range(5))
            wb_d, wT_d, mx_d, rs_d, dg_d, tg_d = ([Dep(), Dep(), Dep(), Dep()] for _ in range(6))
            kr_d, cs_d, mask_d, t1_d, t2_d, wsw_d = (Dep() for _ in range(6))
            sc_d = [Dep() for _ in range(9)]
            op("sp", lambda E: E.dma_start(out=cosT[:], in_=cosT_in), W=[cs_d], dma=misc_sem)
            op("sp", lambda E: E.dma_start(out=sinT[:], in_=sinT_in), W=[cs_d], dma=misc_sem)
            op("sp", lambda E: E.dma_start(out=mb1[:], in_=mb1_in.rearrange("k p n -> p k n")),
               W=[mask_d], dma=misc_sem)
            for r in range(4):
                for par in range(2):
                    nb0 = r if par == 0 else 7 - r
                    src = krall[r * 64:(r + 1) * 64, :].rearrange("p (s two n) -> p two s n", two=2, n=128)[:, par, :, :]
                    dst = kr[:, :].rearrange("p (s e n) -> p e s n", e=8, n=128)[:, nb0, :, :]
                    op("sp", lambda E, src=src, dst=dst: E.dma_start(out=dst, in_=src),
                       R=[krall_d], W=[kr_d], dma=kr_sem)
            hw = {}

            def head_load(h):
                hb = h % 2
                load_kv_head(1, h, kh[hb], kh_d[hb], vh[hb], vh_d[hb], ld_sem[hb])
                hw[h] = (load_w(w_uq[:, h * 192:(h + 1) * 192], 4, 192),
                         load_w(w_in_b[:, 512 + h * 128: 512 + (h + 1) * 128], KC, 128))

            def head_proj(h):
                hb = h % 2
                (wt, wd), (wt2, wd2) = hw.pop(h)
                build_swap(wsw, wsw_d, wt, wd, 4, 128)
                proj_fm(wt, wd, 4, 0, 128, cqT, cqT_d, qn[hb], qn_d[hb])
                rope_fm(wt, wd, 4, 128, wsw, wsw_d, cqT, cqT_d, cosT, sinT, cs_d, t1, t1_d, t2, t2_d,
                        qr[hb], qr_d[hb])
                proj_fm(wt2, wd2, KC, 0, 128, A, A_d, gT[hb], gT_d[hb], func=AF.Silu)

            def ph1(sl, c, r, i):
                h, s, nt, mi = sl["h"], sl["s"], sl["nt"], sl["mi"]
                hb = h % 2
                sl_ = slice(s * 128, (s + 1) * 128)
                ks = slice(c * 512, (c + 1) * 512)
                p, p_d = next_ps(0, 3)
                op("pe", lambda E: E.matmul(p[:], lhsT=qn[hb][:, sl_], rhs=kh[hb][:, ks], start=True, stop=False),
                   R=[qn_d[hb], kh_d[hb]], W=[p_d])
                op("pe", lambda E: E.matmul(p[:], lhsT=qr[hb][0:64, sl_], rhs=kr[0:64, ks], start=False, stop=True),
                   R=[qr_d[hb], kr_d], W=[p_d])
                if c == nt - 1:
                    par = s % 2
                    op("dve", lambda E: E.tensor_tensor(out=scb[:, r, :], in0=p[:], in1=mb1[:, par, :], op=ALU.add),
                       R=[p_d, mask_d], W=[sc_d[r]])
                else:
                    op("act", lambda E: E.activation(out=scb[:, r, :], in_=p[:], func=AF.Copy),
                       R=[p_d], W=[sc_d[r]])
                op("dve", lambda E: E.tensor_reduce(out=mxt[mi][:, c:c + 1], in_=scb[:, r, :], axis=AX.X, op=ALU.max),
                   R=[sc_d[r]], W=[mx_d[mi]])

            def mid(sl):
                nt, mi = sl["nt"], sl["mi"]
                m, m_d = stat_col()
                nb_, nb_d = stat_col()
                op("dve", lambda E: E.tensor_reduce(out=m, in_=mxt[mi][:, 0:nt], axis=AX.X, op=ALU.max),
                   R=[mx_d[mi]], W=[m_d])
                op("dve", lambda E: E.tensor_scalar(out=nb_, in0=m, scalar1=-SC_B, scalar2=None, op0=ALU.mult),
                   R=[m_d], W=[nb_d])
                sl["nb"] = (nb_, nb_d)
                sl["po"] = next_ps(3, 5)

            def stC(sl, c, r, i):
                mi = sl["mi"]
                k3 = i % 3
                nb_, nb_d = sl["nb"]
                op("act", lambda E: E.activation(out=wbt[k3][:], in_=scb[:, r, :], func=AF.Exp, bias=nb_, scale=SC_B,
                                                 accum_out=rst[mi][:, c:c + 1]),
                   R=[sc_d[r], nb_d], W=[wb_d[k3], rs_d[mi]])

            def stD(sl, c, r, i):
                k3 = i % 3
                pv_T(wbt[k3], wb_d[k3], wTt[k3], wT_d[k3])

            def stE(sl, c, r, i):
                h, s, nt, mi = sl["h"], sl["s"], sl["nt"], sl["mi"]
                hb = h % 2
                sb = mi % 2
                sl_ = slice(s * 128, (s + 1) * 128)
                k3 = i % 3
                po, po_d = sl["po"]
                pv_mm(wTt[k3], wT_d[k3], vh[hb], vh_d[hb], po, po_d, c, nt)
                if c == nt - 1:
                    rs, rs1_d = stat_col()
                    rr, rr_d = stat_col()
                    op("dve", lambda E: E.tensor_reduce(out=rs, in_=rst[mi][:, 0:nt], axis=AX.X, op=ALU.add),
                       R=[rs_d[mi]], W=[rs1_d])
                    op("dve", lambda E: E.reciprocal(out=rr, in_=rs), R=[rs1_d], W=[rr_d])
                    op("dve", lambda E: E.tensor_scalar(out=dg[sb][:], in0=identf[:], scalar1=rr, scalar2=None,
                                                        op0=ALU.mult), R=[rr_d, const_d], W=[dg_d[sb]])
                    prb, prb_d = next_ps(5, 6)
                    op("pe", lambda E: E.matmul(prb[:, 0:128], lhsT=onesf[:], rhs=dg[sb][:], start=True, stop=True),
                       R=[dg_d[sb], const_d], W=[prb_d])
                    op("dve", lambda E: E.tensor_tensor(out=tg[sb][:], in0=prb[:, 0:128], in1=gT[hb][:, sl_],
                                                        op=ALU.mult), R=[prb_d, gT_d[hb]], W=[tg_d[sb]])
                    op("dve", lambda E: E.tensor_tensor(out=B[:, h, sl_], in0=po[:, 0:128], in1=tg[sb][:],
                                                        op=ALU.mult), R=[po_d, tg_d[sb]], W=[B_d])

            def run_round(cur, prev):
                ct = cur["tiles"] if cur else []
                pt_ = prev["tiles"] if prev else []
                if prev is not None:
                    for sl in prev["slots"]:
                        mid(sl)
                n = max(len(ct), len(pt_))
                for t in range(-1, n + 2):
                    if 0 <= t + 1 < len(pt_):
                        stC(*pt_[t + 1])
                    if 0 <= t < len(ct):
                        ph1(*ct[t])
                    if 0 <= t < len(pt_):
                        stD(*pt_[t])
                    if 0 <= t - 1 < len(pt_):
                        stE(*pt_[t - 1])

            prev = None
            head_load(0)
            head_proj(0)
            gi = 0
            rnd = 0
            for h in range(H):
                for ri, pair in enumerate(((0, 7), (1, 6), (2, 5), (3, 4))):
                    if h in (0, 2, 4) and ri == 2:
                        emit_deferred(1, 1 + h // 2, 0)
                    if h in (1, 3, 5) and ri == 0:
                        emit_deferred(1, 1 + h // 2, 1)
                    if ri == 1 and h + 1 < H:
                        head_load(h + 1)
                    if ri == 3 and h + 1 < H:
                        head_proj(h + 1)
                    if ri == 0 and h == H - 1:
                        prefetch_w(("out", 1, 0), w_out_b[:, 0:512], KC, 512)
                        prefetch_w(("out", 1, 1), w_out_b[:, 512:1024], KC, 512)
                    slots = []
                    r = 0
                    for si_, s_ in enumerate(pair):
                        slots.append({"h": h, "s": s_, "nt": s_ + 1, "r0": r, "mi": (rnd % 2) * 2 + si_})
                        r += s_ + 1
                    tiles = []
                    for sl in slots:
                        for c in range(sl["nt"]):
                            tiles.append((sl, c, sl["r0"] + c, gi))
                            gi += 1
                    cur = {"slots": slots, "tiles": tiles}
                    run_round(cur, prev)
                    prev = cur
                    rnd += 1
            run_round(None, prev)
            sc.end_phase()

        def dump(src_rows_fn):
            pass

        cqT_d = Dep()
        P01()
        if upto >= 3:
            P3()
        if upto >= 4:
            with ExitStack() as es0:
                ybuf = es0.enter_context(nc.sbuf_tensor("ybuf0", [128, NB, D], F32))
                y_d = [Dep() for _ in range(NB)]
                with ExitStack() as es:
                    P_out(0, w_out_a, ybuf, y_d, x_own, es)
                    sc.end_phase()
                if upto >= 5:
                    with ExitStack() as es:
                        P_ple(0, ybuf, y_d, es)
                        prefetch_w(("lat", "kv"), w_dkv[:, 0:512], KC, 512)
                        xs_d = Dep()
                        store_rows(xs, ybuf, y_d, st_sem[1], xs_d)
                        if debug and upto == 5:
                            store_rows(dbg, ybuf, y_d, st_sem[2])
                        sc.end_phase()
                elif debug:
                    store_rows(dbg, ybuf, y_d, st_sem[2])
                    sc.end_phase()
                if upto >= 6:
                    with ExitStack() as es:
                        P6a(ybuf, y_d, es)
                        sc.end_phase()
        if upto >= 6:
            with ExitStack() as es:
                P6b(es)
                sc.end_phase()
        if upto >= 9:
            P9()
        if upto >= 10:
            with ExitStack() as es0:
                ybuf = es0.enter_context(nc.sbuf_tensor("ybuf1", [128, NB, D], F32))
                y_d = [Dep() for _ in range(NB)]
                with ExitStack() as es:
                    P_out(1, w_out_b, ybuf, y_d, xs, es)
                    sc.end_phase()
                with ExitStack() as es:
                    P_ple(1, ybuf, y_d, es)
                    store_rows(out_own, ybuf, y_d, st_sem[1])
                    sc.end_phase()
        elif not debug or upto < 4:
            pass
        sc.end_phase()
        print("instructions issued:", sc.ninst, {e: sc.prog[e].cnt for e in sc.ENG})
    return nc


def host_prep(inputs):
    x = np.ascontiguousarray(inputs["x"], dtype=np.float32)
    p = np.ascontiguousarray(inputs["p"], dtype=np.float32)
    in_maps = []
    inv = (1.0 / (10000.0 ** (np.arange(0, 64, 2, dtype=np.float32) / np.float32(64)))).astype(np.float32)
    jj = np.arange(512)[None, :]
    ii = np.arange(128)[:, None]
    ident = np.eye(128, dtype=np.float32)
    shared = {}
    for k in ("norm_pre", "norm_post", "kv_norm", "w_dkv", "kv_latent_norm", "w_uk", "w_uv",
              "w_ple_proj", "w_ple_gate"):
        shared[k] = np.ascontiguousarray(inputs[k], dtype=np.float32)
    for k in ("w_in_a", "w_out_a", "w_in_b", "q_latent_norm", "w_uq", "w_out_b"):
        shared[k] = np.ascontiguousarray(inputs[k][0], dtype=np.float32)
    shared["ident"] = ident
    toks = []
    for c in range(8):
        b, r = c // 4, c % 4
        blocks = [qblock(r, s) for s in range(NB)]
        tok = np.concatenate([np.arange(q * 128, (q + 1) * 128) for q in blocks])
        toks.append((b, tok))
        m = dict(shared)
        m["x_own"] = np.ascontiguousarray(x[b, tok, :])
        m["pT"] = np.ascontiguousarray(np.transpose(p[:, b, tok, :], (0, 2, 1)))
        m01 = np.zeros((2, 128, 512), np.float32)
        mb1 = np.zeros((2, 128, 512), np.float32)
        for par in range(2):
            o = r if par == 0 else 3 - r
            m01[par] = (jj < o * 128 + ii).astype(np.float32)
            mb1[par] = np.where((jj // 64) <= ((o * 128 + ii) // 64), 0.0, -1.0e5).astype(np.float32)
        m["m01"] = m01
        m["mneg"] = ((m01 - 1.0) * 30000.0).astype(np.float32)
        m["mb1"] = mb1
        inv64 = 1.0 / (10000.0 ** (np.arange(0, 64, 2, dtype=np.float64) / 64.0))
        ang = tok.astype(np.float64)[:, None] * inv64[None, :]
        cos = np.cos(ang).astype(np.float32).T
        sin = np.sin(ang).astype(np.float32).T
        m["cosT"] = np.ascontiguousarray(np.concatenate([cos, cos], 0))
        m["sinT"] = np.ascontiguousarray(np.concatenate([-sin, sin], 0))
        in_maps.append(m)
    return in_maps, toks


_CACHE = {}


def kernel(**inputs):
    in_maps, toks = host_prep(inputs)
    if "nc" not in _CACHE:
        _CACHE["nc"] = build_program()
    nc = _CACHE["nc"]
    res = run_bass_kernel_spmd(nc, in_maps, core_ids=list(range(8)))
    out = np.empty((2, SEQ, D), np.float32)
    for c in range(8):
        b, tok = toks[c]
        out[b, tok, :] = res.results[c]["out_own"]
    return out
```

```python
import os
import numpy as np
from contextlib import ExitStack
import concourse.bass as bass
import concourse.mybir as mybir
from concourse.bass_utils import run_bass_kernel_spmd

F32 = mybir.dt.float32
BF16 = mybir.dt.bfloat16
AF = mybir.ActivationFunctionType
ALU = mybir.AluOpType
AX = mybir.AxisListType

D = 2048
SEQ = 4096
T = 1024
NB = 8
KC = 16
H = 16
EPS = 1e-6
NEG = -30000.0
GROUPS = [[0, 1, 2, 3], [4, 5, 6, 7]]
SC_A = 128 ** -0.5
SC_B = 192 ** -0.5


def qblock(r, s):
    return 8 * (s // 2) + (r if s % 2 == 0 else 7 - r)


class Dep:
    __slots__ = ("w", "r")

    def __init__(self):
        self.w = None
        self.r = {}


class SemObj:
    def __init__(self, h):
        self.h = h
        self.cnt = 0
        self.deps = []


class Sched:
    ENG = ("pe", "act", "dve", "pool", "sp")

    def __init__(self, nc, block):
        self.nc = nc
        self.block = block
        self.prog = {e: SemObj(nc.alloc_semaphore("prog_" + e)) for e in self.ENG}
        self.q = {e: [] for e in self.ENG}
        self.seen = {e: {} for e in self.ENG}
        self.dsems = []
        self.rr = 0
        self.ninst = 0
        self.stop_at = int(os.environ.get("KSTOP", "1000000000"))
        self.trace = False
        self.names = []

    def dsem(self, name):
        s = SemObj(self.nc.alloc_semaphore(name))
        self.dsems.append(s)
        return s

    def _wait(self, eng, sem, val):
        if val <= 0:
            return
        seen = self.seen[eng]
        if seen.get(sem, 0) >= val:
            return
        seen[sem] = val
        h = sem.h
        self.q[eng].append(lambda E, h=h, val=val: E.wait_ge(h, val))

    def op(self, eng, fn, R=(), W=(), dma=None, inc=None):
        own = self.prog[eng]
        if self.ninst >= self.stop_at:
            return
        self.ninst += 1
        if self.trace:
            self.names.append((self.ninst, eng))
        for d in R:
            if d.w is not None:
                self._wait(eng, d.w[0], d.w[1])
        strict = eng != "pe"
        for d in W:
            if d.w is not None and (strict or d.w[0] is not own):
                self._wait(eng, d.w[0], d.w[1])
            for sem, val in d.r.items():
                if strict or sem is not own:
                    self._wait(eng, sem, val)
        if dma is not None:
            dma.cnt += 16
            tok = (dma, dma.cnt)
            h = dma.h
            self.q[eng].append(lambda E, fn=fn, h=h: fn(E).then_inc(h, 16))
        elif inc is not None:
            inc.cnt += 1
            tok = (inc, inc.cnt)
            h = inc.h
            self.q[eng].append(lambda E, fn=fn, h=h: fn(E).then_inc(h, 1))
        else:
            own.cnt += 1
            tok = (own, own.cnt)
            h = own.h
            self.q[eng].append(lambda E, fn=fn, h=h: fn(E).then_inc(h, 1))
        for d in R:
            if d.r.get(tok[0], 0) < tok[1]:
                d.r[tok[0]] = tok[1]
        for d in W:
            d.w = tok
            d.r = {}
        if dma is not None:
            keep = []
            for d in dma.deps:
                if d.w is not None and d.w[0] is dma:
                    d.w = tok
                    keep.append(d)
            for d in W:
                if d not in keep:
                    keep.append(d)
            dma.deps = keep

    def wait_for_write(self, eng, W):
        for d in W:
            if d.w is not None:
                self._wait(eng, d.w[0], d.w[1])
            for sem, val in d.r.items():
                self._wait(eng, sem, val)

    def set_writer(self, W, sem):
        tok = (sem, sem.cnt)
        for d in W:
            d.w = tok
            d.r = {}
            if d not in sem.deps:
                sem.deps.append(d)

    def barrier(self):
        for e in self.ENG:
            for e2 in self.ENG:
                self._wait(e, self.prog[e2], self.prog[e2].cnt)
            for s in self.dsems:
                self._wait(e, s, s.cnt)

    def flush(self):
        m = {"pe": self.block.tensor, "act": self.block.scalar, "dve": self.block.vector,
             "pool": self.block.gpsimd, "sp": self.block.sync}
        for e in self.ENG:
            lst = self.q[e]
            if not lst:
                continue
            self.q[e] = []

            def body(E, lst=lst):
                for f in lst:
                    f(E)
            m[e](body)

    def end_phase(self):
        self.barrier()
        self.flush()

    def alt(self):
        self.rr ^= 1
        return "act" if self.rr else "dve"


def build_program(upto=99, debug=False):
    nc = bass.Bass("TRN2", target_bir_lowering=False)

    def din(name, shape, dt=F32):
        return nc.dram_tensor(name, list(shape), dt, kind="ExternalInput").ap()

    def dint(name, shape, dt=BF16):
        return nc.dram_tensor(name, list(shape), dt, kind="Internal").ap()

    x_own = din("x_own", [T, D])
    pT_in = din("pT", [2, 256, T])
    norm_pre = din("norm_pre", [2, D])
    norm_post = din("norm_post", [2, D])
    w_in_a = din("w_in_a", [D, 4 * D])
    w_out_a = din("w_out_a", [D, D])
    w_in_b = din("w_in_b", [D, 2560])
    q_lat_norm = din("q_latent_norm", [512])
    w_uq = din("w_uq", [512, 3072])
    w_out_b = din("w_out_b", [D, D])
    kv_norm = din("kv_norm", [D])
    w_dkv = din("w_dkv", [D, 576])
    kv_lat_norm = din("kv_latent_norm", [512])
    w_uk = din("w_uk", [512, D])
    w_uv = din("w_uv", [512, D])
    w_pp = din("w_ple_proj", [2, 256, D])
    w_pg = din("w_ple_gate", [2, D, D])
    m01_in = din("m01", [2, 128, 512])
    mneg_in = din("mneg", [2, 128, 512])
    mb1_in = din("mb1", [2, 128, 512])
    cosT_in = din("cosT", [64, T])
    sinT_in = din("sinT", [64, T])
    ident_in = din("ident", [128, 128])
    out_own = nc.dram_tensor("out_own", [T, D], F32, kind="ExternalOutput").ap()
    dbg = nc.dram_tensor("dbg", [T, D], F32, kind="ExternalOutput").ap() if debug else None

    kin = [[dint("kin%d_%d" % (l, g), [512, T]) for g in range(4)] for l in range(2)]
    kall = [[dint("kall%d_%d" % (l, g), [4 * 512, T]) for g in range(4)] for l in range(2)]
    vin = [[dint("vin%d_%d" % (l, g), [T, 512]) for g in range(4)] for l in range(2)]
    vall = [[dint("vall%d_%d" % (l, g), [4 * T, 512]) for g in range(4)] for l in range(2)]
    krin = dint("krin", [64, T])
    krall = dint("krall", [4 * 64, T])
    xs = dint("xs", [T, D], F32)

    A = nc.alloc_sbuf_tensor("bufA", [128, KC, T], BF16)
    B = nc.alloc_sbuf_tensor("bufB", [128, KC, T], BF16)
    A_d, B_d = Dep(), Dep()
    WN = 8192
    wbuf = [nc.alloc_sbuf_tensor("wbuf%d" % i, [128, WN], BF16) for i in range(2)]
    identb = nc.alloc_sbuf_tensor("identb", [128, 128], BF16)
    identf = nc.alloc_sbuf_tensor("identf", [128, 128], F32)
    onesf = nc.alloc_sbuf_tensor("onesf", [128, 128], F32)
    stats = nc.alloc_sbuf_tensor("stats", [128, 512], F32)
    cqT = nc.alloc_sbuf_tensor("cqT", [128, 4, T], BF16)
    ps = [nc.alloc_psum_tensor("ps%d" % i, [128, 512], F32) for i in range(6)]
    psb = [nc.alloc_psum_tensor("psb%d" % i, [128, 1024], BF16) for i in range(2)]

    with nc.Block() as block:
        sc = Sched(nc, block)
        op = sc.op
        ps_d = [Dep() for _ in ps]
        psb_d = [Dep() for _ in psb]
        wbuf_d = [Dep() for _ in wbuf]
        wsem = [sc.dsem("wsem%d" % i) for i in range(2)]
        ld_sem = [sc.dsem("ld%d" % i) for i in range(4)]
        st_sem = [sc.dsem("st%d" % i) for i in range(4)]
        misc_sem = sc.dsem("misc")
        pm_sem = sc.dsem("pmisc")
        kr_sem = sc.dsem("krld")
        whw_sem = sc.dsem("whw")
        cc_sem = SemObj(nc.alloc_semaphore("ccsem"))
        const_d = Dep()
        state = {"w": 0, "ps": 0, "psb": 0, "stat": 0}

        def next_ps(lo=0, hi=6):
            key = ("ps", lo, hi)
            i = state.get(key, lo)
            state[key] = lo + (i + 1 - lo) % (hi - lo)
            return ps[i], ps_d[i]

        def next_psb():
            i = state["psb"]
            state["psb"] = (i + 1) % len(psb)
            return psb[i], psb_d[i]

        stat_deps = [Dep() for _ in range(512)]

        def stat_col():
            i = state["stat"]
            state["stat"] = (i + 1) % 512
            return stats[:, i:i + 1], stat_deps[i]

        prefetched = {}

        def prefetch_w(key, src2d, kc, n):
            prefetched[key] = load_w(src2d, kc, n)

        def load_w(src2d, kc, n, key=None):
            if key is not None and key in prefetched:
                return prefetched.pop(key)
            i = state["w"]
            state["w"] = 1 - i
            dst = wbuf[i][:, 0:kc * n].rearrange("p (c n) -> p c n", n=n)
            src = src2d.rearrange("(c p) n -> p c n", p=128)
            stg = state.get("stg")
            if stg is not None and i == 1 and kc * n == WN:
                sg, sg_d = stg
                op("sp", lambda E: E.dma_start(out=sg[:, 0:kc * n].rearrange("p (c n) -> p c n", n=n), in_=src),
                   W=[sg_d], dma=whw_sem)
                hf = kc * n // 2
                op("act", lambda E: E.activation(out=wbuf[i][:, 0:hf], in_=sg[:, 0:hf], func=AF.Copy),
                   R=[sg_d], W=[wbuf_d[i]])
                op("dve", lambda E: E.tensor_copy(out=wbuf[i][:, hf:2 * hf], in_=sg[:, hf:2 * hf]),
                   R=[sg_d], W=[wbuf_d[i]])
                return dst, wbuf_d[i]
            op("pool", lambda E: E.dma_start(out=dst, in_=src), W=[wbuf_d[i]], dma=wsem[i])
            return dst, wbuf_d[i]

        def evac(out, in_, R, W, eng=None):
            e = eng or sc.alt()
            if e == "act":
                op(e, lambda E: E.activation(out=out, in_=in_, func=AF.Copy), R=R, W=W)
            else:
                op(e, lambda E: E.tensor_copy(out=out, in_=in_), R=R, W=W)

        def load_bcast(dst, vec_ap, d):
            op("sp", lambda E: E.dma_start(out=dst[:], in_=vec_ap.partition_broadcast(128)),
               W=[d], dma=misc_sem)

        def rstd_from_ssq(ssq, ssq_d, n):
            a, a_d = stat_col()
            b, b_d = stat_col()
            c, c_d = stat_col()
            op("dve", lambda E: E.tensor_scalar(out=a, in0=ssq, scalar1=1.0 / n, scalar2=EPS,
                                                op0=ALU.mult, op1=ALU.add), R=[ssq_d], W=[a_d])
            op("act", lambda E: E.activation(out=b, in_=a, func=AF.Sqrt), R=[a_d], W=[b_d])
            op("dve", lambda E: E.reciprocal(out=c, in_=b), R=[b_d], W=[c_d])
            return c, c_d

        def transpose_block(src, src_d, ncols, dst, dst_d, b):
            nch = ncols // 128
            for g0 in range(0, nch, 4):
                gn = min(4, nch - g0)
                pt, pt_d = next_psb()
                for j in range(gn):
                    c = g0 + j
                    op("pe", lambda E, c=c, j=j, pt=pt: E.transpose(
                        out=pt[:, j * 128:(j + 1) * 128], in_=src[:, c * 128:(c + 1) * 128],
                        identity=identb[:]), R=[src_d, const_d], W=[pt_d])
                o = dst[:, g0:g0 + gn, b * 128:(b + 1) * 128]
                i_ = pt[:, 0:gn * 128].rearrange("p (c n) -> p c n", n=128)
                evac(o, i_, [pt_d], [dst_d])

        def norm_transpose(get_x, gains, outs, junk, junk_d, xn, xn_d):
            for b in range(NB):
                xa, xa_d = get_x(b)
                ssq, ssq_d = stat_col()
                op("act", lambda E, xa=xa, ssq=ssq: E.activation(out=junk[:], in_=xa, func=AF.Square,
                                                                 accum_out=ssq),
                   R=[xa_d], W=[junk_d, ssq_d])
                r, r_d = rstd_from_ssq(ssq, ssq_d, D)
                for gi, ((gb, gb_d), (dst, dst_d)) in enumerate(zip(gains, outs)):
                    k = (b * len(gains) + gi) % 2
                    op("dve", lambda E, xa=xa, r=r, gb=gb, k=k: E.scalar_tensor_tensor(
                        out=xn[k][:], in0=xa, scalar=r, in1=gb[:], op0=ALU.mult, op1=ALU.mult),
                       R=[xa_d, r_d, gb_d], W=[xn_d[k]])
                    transpose_block(xn[k], xn_d[k], D, dst, dst_d, b)

        deferred = {0: {}, 1: {}}

        def gather(src, dst, src_ds, dst_d, defer=None):
            if defer is not None:
                deferred[defer[0]].setdefault(defer[1], []).append((src, dst, list(src_ds), dst_d))
                return
            op("pool", lambda E: E.collective_compute("AllGather", ALU.bypass, replica_groups=GROUPS,
                                                      ins=[src], outs=[dst]), R=src_ds, W=[dst_d], inc=cc_sem)

        def emit_deferred(l, cg, which):
            lst = deferred[l].get(cg, [])
            if which < len(lst) and lst[which] is not None:
                src, dst, src_ds, dst_d = lst[which]
                lst[which] = None
                gather(src, dst, src_ds, dst_d)

        def proj_fm(wt, wd, kc, col0, m, src, src_d, dst, dst_d, func=None):
            for half in range(2):
                p, p_d = next_ps()
                for c in range(kc):
                    op("pe", lambda E, c=c, p=p, half=half: E.matmul(
                        p[0:m, :], lhsT=wt[:, c, col0:col0 + m], rhs=src[:, c, half * 512:(half + 1) * 512],
                        start=(c == 0), stop=(c == kc - 1)), R=[wd, src_d], W=[p_d])
                o = dst[0:m, half * 512:(half + 1) * 512]
                if func is None:
                    evac(o, p[0:m, :], [p_d], [dst_d])
                else:
                    op("act", lambda E, o=o, p=p: E.activation(out=o, in_=p[0:m, :], func=func),
                       R=[p_d], W=[dst_d])

        op("pool", lambda E: E.dma_start(out=identb[:], in_=ident_in), W=[const_d], dma=pm_sem)
        op("sp", lambda E: E.dma_start(out=identf[:], in_=ident_in), W=[const_d], dma=misc_sem)
        op("pool", lambda E: E.memset(onesf[:], 1.0), W=[const_d])
        sc.end_phase()

        kall_d = [[Dep() for _ in range(4)] for _ in range(2)]
        vall_d = [[Dep() for _ in range(4)] for _ in range(2)]
        krall_d = Dep()

        def P01():
          with ExitStack() as es:
            def al(name, shape, dt):
                return es.enter_context(nc.sbuf_tensor(name, shape, dt))
            xt = [al("xt%d" % i, [128, D], F32) for i in range(2)]
            junk = al("junk", [128, D], BF16)
            xn = [al("xn%d" % i, [128, D], BF16) for i in range(2)]
            gb0 = al("gb0", [128, D], F32)
            kst = [al("kst%d" % i, [128, T], BF16) for i in range(2)]
            vst = [al("vst%d" % i, [128, 512], BF16) for i in range(2)]
            stg01 = al("stg01", [128, WN], F32)
            state["stg"] = (stg01[:, :], Dep())
            xt_d = [Dep(), Dep()]
            gb0_d = Dep()
            load_bcast(gb0, norm_pre[0], gb0_d)

            def get_x0(b):
                k = b % 2
                op("sp", lambda E: E.dma_start(out=xt[k][:], in_=x_own[b * 128:(b + 1) * 128, :]),
                   W=[xt_d[k]], dma=ld_sem[k])
                return xt[k][:], xt_d[k]

            norm_transpose(get_x0, [(gb0, gb0_d)], [(A, A_d)], junk, Dep(), xn, [Dep(), Dep()])
            kst_d = [Dep(), Dep()]
            vst_d = [Dep(), Dep()]
            kin_d = [Dep() for _ in range(H)]
            vin_d = [Dep() for _ in range(32)]
            i = 0
            steps = []
            for cg in range(4):
                steps.append(("k", cg, w_in_a[:, D + cg * 512: D + (cg + 1) * 512]))
                steps.append(("v", cg, w_in_a[:, 2 * D + cg * 512: 2 * D + (cg + 1) * 512]))
            loaded = {0: load_w(steps[0][2], KC, 512)}
            for si, (kind, cg, _) in enumerate(steps):
                wt, wd = loaded.pop(si)
                if si + 1 < len(steps):
                    loaded[si + 1] = load_w(steps[si + 1][2], KC, 512)
                if kind == "k":
                    for hh in range(4):
                        h = cg * 4 + hh
                        k = h % 2
                        proj_fm(wt, wd, KC, hh * 128, 128, A, A_d, kst[k], kst_d[k])
                        op("sp", lambda E, hh=hh, cg=cg, k=k: E.dma_start(
                            out=kin[0][cg][hh * 128:(hh + 1) * 128, :], in_=kst[k][:]),
                           R=[kst_d[k]], W=[kin_d[h]], dma=st_sem[k])
                    gather(kin[0][cg], kall[0][cg], kin_d[cg * 4:cg * 4 + 4], kall_d[0][cg],
                           defer=(0, cg) if cg > 0 else None)
                else:
                    for b in range(NB):
                        k = i % 2
                        i += 1
                        p, p_d = next_ps()
                        for c in range(KC):
                            op("pe", lambda E, c=c, p=p, wt=wt, b=b: E.matmul(
                                p[:], lhsT=A[:, c, b * 128:(b + 1) * 128], rhs=wt[:, c, :],
                                start=(c == 0), stop=(c == KC - 1)), R=[wd, A_d], W=[p_d])
                        evac(vst[k][:], p[:], [p_d], [vst_d[k]])
                        op("sp", lambda E, b=b, cg=cg, k=k: E.dma_start(
                            out=vin[0][cg][b * 128:(b + 1) * 128, :], in_=vst[k][:]),
                           R=[vst_d[k]], W=[vin_d[cg * NB + b]], dma=st_sem[2 + k])
                    gather(vin[0][cg], vall[0][cg], vin_d[cg * NB:(cg + 1) * NB], vall_d[0][cg],
                           defer=(0, cg) if cg > 0 else None)
            state["stg"] = None
            sc.end_phase()

        def load_kv_head(l, h, kh, kh_d, vh, vh_d, sem):
            sc.wait_for_write("sp", [kh_d, vh_d])
            for r in range(4):
                for par in range(2):
                    nb0 = r if par == 0 else 7 - r
                    src = kall[l][h // 4][(r * 4 + h % 4) * 128:(r * 4 + h % 4 + 1) * 128, :].rearrange(
                        "p (s two n) -> p two s n", two=2, n=128)[:, par, :, :]
                    dst = kh[:, :].rearrange("p (s e n) -> p e s n", e=8, n=128)[:, nb0, :, :]
                    op("sp", lambda E, src=src, dst=dst: E.dma_start(out=dst, in_=src),
                       R=[kall_d[l][h // 4]], W=[], dma=sem)
                    srcv = vall[l][h // 4][r * T:(r + 1) * T, (h % 4) * 128:(h % 4 + 1) * 128].rearrange(
                        "(s two p) n -> p two s n", two=2, p=128)[:, par, :, :]
                    dstv = vh[:, :, :].rearrange("p (s e) n -> p e s n", e=8)[:, nb0, :, :]
                    op("sp", lambda E, srcv=srcv, dstv=dstv: E.dma_start(out=dstv, in_=srcv),
                       R=[vall_d[l][h // 4]], W=[], dma=sem)
            sc.set_writer([kh_d, vh_d], sem)

        def pv_T(wbt_k, wb_dk, wTt_k, wT_dk):
            pt, pt_d = next_psb()
            for j in range(4):
                op("pe", lambda E, j=j, pt=pt: E.transpose(
                    out=pt[:, j * 128:(j + 1) * 128], in_=wbt_k[:, j * 128:(j + 1) * 128],
                    identity=identb[:]), R=[wb_dk, const_d], W=[pt_d])
            evac(wTt_k[:], pt[:, 0:512], [pt_d], [wT_dk], eng="dve")

        def pv_mm(wTt_k, wT_dk, vh_h, vh_dh, po, po_d, c, nt):
            for j in range(4):
                kb = c * 4 + j
                op("pe", lambda E, j=j, kb=kb: E.matmul(
                    po[:, 0:128], lhsT=vh_h[:, kb, :], rhs=wTt_k[:, j * 128:(j + 1) * 128],
                    start=(kb == 0), stop=(kb == 4 * nt - 1)),
                   R=[vh_dh, wT_dk], W=[po_d])

        def P3():
          with ExitStack() as es:
            def al(name, shape, dt):
                return es.enter_context(nc.sbuf_tensor(name, shape, dt))
            kh = [al("kh%d" % i, [128, SEQ], BF16) for i in range(2)]
            vh = [al("vh%d" % i, [128, 32, 128], BF16) for i in range(2)]
            qT = [al("qT%d" % i, [128, T], BF16) for i in range(2)]
            gT = [al("gT%d" % i, [128, T], BF16) for i in range(2)]
            eb = al("eb", [128, 9, 512], F32)
            Pb = al("Pb", [128, 9 * 513 + 8], F32)
            spt = [al("spt%d" % i, [128, 520], F32) for i in range(2)]
            ut = [al("ut%d" % i, [128, 512], F32) for i in range(2)]
            wbt = [al("wb%d" % i, [128, 512], BF16) for i in range(3)]
            wTt = [al("wT%d" % i, [128, 512], BF16) for i in range(3)]
            m01 = al("m01s", [128, 2, 512], F32)
            ones513 = al("ones513", [128, 520], F32)
            kh_d, vh_d, qT_d, gT_d = ([Dep(), Dep()] for _ in range(4))
            spt_d, ut_d, wb_d, wT_d = ([Dep(), Dep(), Dep()] for _ in range(4))
            e_d = [Dep() for _ in range(9)]
            P_d = [Dep() for _ in range(10)]
            mask_d = Dep()
            op("sp", lambda E: E.dma_start(out=m01[:], in_=m01_in.rearrange("k p n -> p k n")),
               W=[mask_d], dma=misc_sem)
            op("pool", lambda E: E.memset(ones513[:], 1.0), W=[const_d])
            for k in range(2):
                op("pool", lambda E, k=k: E.memset(spt[k][:, 512:520], 0.0), W=[spt_d[k]])
            cnt = {"a": 0, "b": 0}

            hw = {}

            def head_load(h):
                hb = h % 2
                load_kv_head(0, h, kh[hb], kh_d[hb], vh[hb], vh_d[hb], ld_sem[hb])
                hw[h] = (load_w(w_in_a[:, h * 128:(h + 1) * 128], KC, 128),
                         load_w(w_in_a[:, 3 * D + h * 128: 3 * D + (h + 1) * 128], KC, 128))

            def head_proj(h):
                hb = h % 2
                (wt, wd), (wt2, wd2) = hw.pop(h)
                proj_fm(wt, wd, KC, 0, 128, A, A_d, qT[hb], qT_d[hb])
                proj_fm(wt2, wd2, KC, 0, 128, A, A_d, gT[hb], gT_d[hb], func=AF.Silu)

            def ph1a(sl, c, r):
                h, s, nt = sl["h"], sl["s"], sl["nt"]
                hb = h % 2
                p, p_d = next_ps(0, 4)
                qs = qT[hb][:, s * 128:(s + 1) * 128]
                op("pe", lambda E: E.matmul(p[:], lhsT=qs, rhs=kh[hb][:, c * 512:(c + 1) * 512],
                                            start=True, stop=True), R=[qT_d[hb], kh_d[hb]], W=[p_d])
                op("act", lambda E: E.activation(out=eb[:, r, :], in_=p[:], func=AF.Exp, scale=SC_A),
                   R=[p_d], W=[e_d[r]])
                if c == nt - 1:
                    par = s % 2
                    op("pool", lambda E: E.tensor_tensor(out=eb[:, r, :], in0=eb[:, r, :], in1=m01[:, par, :],
                                                         op=ALU.mult), R=[e_d[r], mask_d], W=[e_d[r]])
                if c == 0:
                    op("pool", lambda E: E.memset(Pb[:, 513 * r:513 * r + 1], 0.0), W=[P_d[r]])

            def ph1b(sl, c, r):
                nt = sl["nt"]
                k = cnt["a"] % 2
                cnt["a"] += 1
                op("act", lambda E: E.activation(out=spt[k][:, 0:512], in_=eb[:, r, :], func=AF.Ln, bias=1.0),
                   R=[e_d[r]], W=[spt_d[k]])
                b0 = 513 * r
                n = 512 if c == nt - 1 else 513
                init = 0.0 if c == 0 else Pb[:, b0:b0 + 1]
                wr = [P_d[r]] + ([P_d[r + 1]] if n == 513 else [])
                op("dve", lambda E: E.tensor_tensor_scan(
                    out=Pb[:, b0 + 1:b0 + 1 + n], data0=ones513[:, 0:n], data1=spt[k][:, 0:n],
                    initial=init, op0=ALU.mult, op1=ALU.add),
                   R=[spt_d[k], const_d, P_d[r]], W=wr)

            def mid(sl):
                rl = sl["r0"] + sl["nt"] - 1
                nP, nP_d = stat_col()
                op("dve", lambda E: E.tensor_scalar(out=nP, in0=Pb[:, 513 * rl + 512:513 * rl + 513], scalar1=-1.0,
                                                    scalar2=None, op0=ALU.mult), R=[P_d[rl]], W=[nP_d])
                sl["nP"] = (nP, nP_d)
                sl["po"] = next_ps(4, 6)

            def stC_act(sl, c, r, i):
                k = i % 2
                nP, nP_d = sl["nP"]
                op("act", lambda E: E.activation(out=ut[k][:], in_=Pb[:, 513 * r:513 * r + 512], func=AF.Exp,
                                                 bias=nP, scale=1.0), R=[P_d[r], nP_d], W=[ut_d[k]])

            def stC_pool(sl, c, r, i):
                k = i % 2
                k3 = i % 3
                op("pool", lambda E: E.tensor_tensor(out=wbt[k3][:], in0=eb[:, r, :], in1=ut[k][:], op=ALU.mult),
                   R=[e_d[r], ut_d[k]], W=[wb_d[k3]])

            def stD(sl, c, r, i):
                k3 = i % 3
                pv_T(wbt[k3], wb_d[k3], wTt[k3], wT_d[k3])

            def stE(sl, c, r, i):
                h, s, nt = sl["h"], sl["s"], sl["nt"]
                hb = h % 2
                k3 = i % 3
                po, po_d = sl["po"]
                pv_mm(wTt[k3], wT_d[k3], vh[hb], vh_d[hb], po, po_d, c, nt)
                if c == nt - 1:
                    op("dve", lambda E: E.tensor_tensor(
                        out=B[:, h, s * 128:(s + 1) * 128], in0=po[:, 0:128],
                        in1=gT[hb][:, s * 128:(s + 1) * 128], op=ALU.mult), R=[po_d, gT_d[hb]], W=[B_d])

            def run_round(cur, prev):
                ct = cur["tiles"] if cur else []
                pt_ = prev["tiles"] if prev else []
                if prev is not None:
                    for sl in prev["slots"]:
                        mid(sl)
                n = max(len(ct), len(pt_))
                for t in range(-1, n + 2):
                    if 0 <= t + 1 < len(pt_):
                        stC_act(*pt_[t + 1])
                    if 0 <= t < len(ct):
                        ph1a(*ct[t])
                    if 0 <= t + 1 < len(pt_):
                        stC_pool(*pt_[t + 1])
                    if 0 <= t < len(pt_):
                        stD(*pt_[t])
                    if 0 <= t - 1 < len(pt_):
                        stE(*pt_[t - 1])
                    if 0 <= t - 1 < len(ct):
                        ph1b(*ct[t - 1])

            def run_round_mixed(cur1, cur, prev):
                run_round(cur1, prev)

            prev = None
            head_load(0)
            head_proj(0)
            gi = 0
            for h in range(H):
                for ri, pair in enumerate(((0, 7), (1, 6), (2, 5), (3, 4))):
                    if h in (0, 2, 4) and ri == 2:
                        emit_deferred(0, 1 + h // 2, 0)
                    if h in (1, 3, 5) and ri == 0:
                        emit_deferred(0, 1 + h // 2, 1)
                    if ri == 1 and h + 1 < H:
                        head_load(h + 1)
                    if ri == 3 and h + 1 < H:
                        head_proj(h + 1)
                    if ri == 0 and h == H - 1:
                        prefetch_w(("out", 0, 0), w_out_a[:, 0:512], KC, 512)
                        prefetch_w(("out", 0, 1), w_out_a[:, 512:1024], KC, 512)
                    slots = []
                    r = 0
                    for s_ in pair:
                        slots.append({"h": h, "s": s_, "nt": s_ + 1, "r0": r})
                        r += s_ + 1
                    tiles = []
                    for sl in slots:
                        for c in range(sl["nt"]):
                            tiles.append((sl, c, sl["r0"] + c))
                    cur = {"slots": slots, "tiles": [(sl, c, r_, gi + i) for i, (sl, c, r_) in enumerate(tiles)]}
                    gi += len(tiles)
                    cur1 = {"slots": slots, "tiles": [(sl, c, r_) for (sl, c, r_, _) in cur["tiles"]]}
                    run_round_mixed(cur1, cur, prev)
                    prev = cur
            run_round_mixed(None, None, prev)
            sc.end_phase()

        def P_out(l, w_out, ybuf, y_d, x_src, es):
            def al(name, shape, dt):
                return es.enter_context(nc.sbuf_tensor(name, shape, dt))
            xt = [al("xo%d_%d" % (l, i), [128, D], F32) for i in range(2)]
            junk = al("junko%d" % l, [128, D], BF16)
            gp = al("gpost%d" % l, [128, D], F32)
            xt_d = [Dep(), Dep()]
            junk_d = Dep()
            gp_d = Dep()
            load_bcast(gp, norm_post[l], gp_d)
            state["stg"] = (A.bitcast(F32)[:, :, :].rearrange("p c n -> p (c n)"), A_d)
            for n in range(4):
                wt, wd = load_w(w_out[:, n * 512:(n + 1) * 512], KC, 512, key=("out", l, n))
                for b in range(NB):
                    p, p_d = next_ps()
                    for c in range(KC):
                        op("pe", lambda E, c=c, p=p, wt=wt, b=b: E.matmul(
                            p[:], lhsT=B[:, c, b * 128:(b + 1) * 128], rhs=wt[:, c, :],
                            start=(c == 0), stop=(c == KC - 1)), R=[wd, B_d], W=[p_d])
                    evac(ybuf[:, b, n * 512:(n + 1) * 512], p[:], [p_d], [y_d[b]])
            prefetch_w(("pg", l, 0), w_pg[l][:, 0:512], KC, 512)
            state["stg"] = None
            for b in range(NB):
                k = b % 2
                op("sp", lambda E, b=b, k=k: E.dma_start(out=xt[k][:], in_=x_src[b * 128:(b + 1) * 128, :]),
                   W=[xt_d[k]], dma=ld_sem[k])
                ssq, ssq_d = stat_col()
                op("act", lambda E, b=b, ssq=ssq: E.activation(out=junk[:], in_=ybuf[:, b, :], func=AF.Square,
                                                               accum_out=ssq), R=[y_d[b]], W=[junk_d, ssq_d])
                r, r_d = rstd_from_ssq(ssq, ssq_d, D)
                op("dve", lambda E, b=b, r=r: E.scalar_tensor_tensor(
                    out=ybuf[:, b, :], in0=ybuf[:, b, :], scalar=r, in1=gp[:], op0=ALU.mult, op1=ALU.mult),
                   R=[y_d[b], r_d, gp_d], W=[y_d[b]])
                op("pool", lambda E, b=b, k=k: E.tensor_tensor(
                    out=ybuf[:, b, :], in0=ybuf[:, b, :], in1=xt[k][:], op=ALU.add),
                   R=[y_d[b], xt_d[k]], W=[y_d[b]])

        def P_ple(l, ybuf, y_d, es):
            def al(name, shape, dt):
                return es.enter_context(nc.sbuf_tensor(name, shape, dt))
            xb16 = [al("xb16_%d_%d" % (l, i), [128, D], BF16) for i in range(2)]
            sgt = [al("sgt%d_%d" % (l, i), [128, 512], F32) for i in range(2)]
            tt = [al("tt%d_%d" % (l, i), [128, 512], F32) for i in range(2)]
            pTt = al("pTt%d" % l, [128, 2, T], BF16)
            wppt = [al("wppt%d_%d" % (l, i), [128, 2, 512], BF16) for i in range(2)]
            xb_d = [Dep(), Dep()]
            sg_d = [Dep(), Dep()]
            tt_d = [Dep(), Dep()]
            pT_d = Dep()
            wpp_d = [Dep(), Dep()]
            op("pool", lambda E: E.dma_start(out=pTt[:], in_=pT_in[l].rearrange("(j p) t -> p j t", p=128)),
               W=[pT_d], dma=pm_sem)
            for b in range(NB):
                k = b % 2
                op("pool", lambda E, b=b, k=k: E.tensor_copy(out=xb16[k][:], in_=ybuf[:, b, :]),
                   R=[y_d[b]], W=[xb_d[k]])
                transpose_block(xb16[k], xb_d[k], D, A, A_d, b)
            i = 0
            state["stg"] = (B.bitcast(F32)[:, :, :].rearrange("p c n -> p (c n)"), B_d)
            for n in range(4):
                wt, wd = load_w(w_pg[l][:, n * 512:(n + 1) * 512], KC, 512, key=("pg", l, n))
                kk = n % 2
                op("pool", lambda E, n=n, kk=kk: E.dma_start(
                    out=wppt[kk][:], in_=w_pp[l][:, n * 512:(n + 1) * 512].rearrange("(j p) n -> p j n", p=128)),
                   W=[wpp_d[kk]], dma=ld_sem[2 + kk])
                for b in range(NB):
                    k = i % 2
                    i += 1
                    pa, pa_d = next_ps()
                    for c in range(KC):
                        op("pe", lambda E, c=c, pa=pa, wt=wt, b=b: E.matmul(
                            pa[:], lhsT=A[:, c, b * 128:(b + 1) * 128], rhs=wt[:, c, :],
                            start=(c == 0), stop=(c == KC - 1)), R=[wd, A_d], W=[pa_d])
                    pb, pb_d = next_ps()
                    for j in range(2):
                        op("pe", lambda E, j=j, pb=pb, b=b, kk=kk: E.matmul(
                            pb[:], lhsT=pTt[:, j, b * 128:(b + 1) * 128], rhs=wppt[kk][:, j, :],
                            start=(j == 0), stop=(j == 1)), R=[pT_d, wpp_d[kk]], W=[pb_d])
                    op("act", lambda E, pa=pa, k=k: E.activation(out=sgt[k][:], in_=pa[:], func=AF.Sigmoid),
                       R=[pa_d], W=[sg_d[k]])
                    op("dve", lambda E, pb=pb, k=k: E.tensor_tensor(out=tt[k][:], in0=pb[:], in1=sgt[k][:],
                                                                    op=ALU.mult), R=[pb_d, sg_d[k]], W=[tt_d[k]])
                    op("pool", lambda E, b=b, n=n, k=k: E.tensor_tensor(
                        out=ybuf[:, b, n * 512:(n + 1) * 512], in0=ybuf[:, b, n * 512:(n + 1) * 512],
                        in1=tt[k][:], op=ALU.add), R=[y_d[b], tt_d[k]], W=[y_d[b]])

        def store_rows(dst, ybuf, y_d, sem, dst_d=None):
            state["stg"] = None
            for b in range(NB):
                op("sp", lambda E, b=b: E.dma_start(out=dst[b * 128:(b + 1) * 128, :], in_=ybuf[:, b, :]),
                   R=[y_d[b]], W=[dst_d] if dst_d is not None else [], dma=sem)

        def latent_norm_T(es, tag, wsrc, src, src_d, gvec, dst, dst_d):
            def al(name, shape, dt):
                return es.enter_context(nc.sbuf_tensor(name, shape, dt))
            gl = al("gl" + tag, [128, 512], F32)
            cn = [al("cn%s%d" % (tag, i), [128, 512], BF16) for i in range(2)]
            jk = al("jk" + tag, [128, 512], BF16)
            gl_d, jk_d = Dep(), Dep()
            cn_d = [Dep(), Dep()]
            load_bcast(gl, gvec, gl_d)
            wt, wd = load_w(wsrc, KC, 512, key=("lat", tag))
            for b in range(NB):
                k = b % 2
                p, p_d = next_ps()
                for c in range(KC):
                    op("pe", lambda E, c=c, p=p, b=b: E.matmul(
                        p[:], lhsT=src[:, c, b * 128:(b + 1) * 128], rhs=wt[:, c, :],
                        start=(c == 0), stop=(c == KC - 1)), R=[wd, src_d], W=[p_d])
                ssq, ssq_d = stat_col()
                op("act", lambda E, p=p, ssq=ssq: E.activation(out=jk[:], in_=p[:], func=AF.Square, accum_out=ssq),
                   R=[p_d], W=[jk_d, ssq_d])
                r, r_d = rstd_from_ssq(ssq, ssq_d, 512)
                op("dve", lambda E, p=p, r=r, k=k: E.scalar_tensor_tensor(
                    out=cn[k][:], in0=p[:], scalar=r, in1=gl[:], op0=ALU.mult, op1=ALU.mult),
                   R=[p_d, r_d, gl_d], W=[cn_d[k]])
                transpose_block(cn[k], cn_d[k], 512, dst, dst_d, b)

        def rope_fm(wt, wd, kc, colA, wsw, wsw_d, src, src_d, cosT, sinT, cs_d, t1, t1_d, t2, t2_d, dst, dst_d):
            for half in range(2):
                pa, pa_d = next_ps()
                pb, pb_d = next_ps()
                for c in range(kc):
                    op("pe", lambda E, c=c, pa=pa, half=half: E.matmul(
                        pa[0:64, :], lhsT=wt[:, c, colA:colA + 64], rhs=src[:, c, half * 512:(half + 1) * 512],
                        start=(c == 0), stop=(c == kc - 1)), R=[wd, src_d], W=[pa_d])
                for c in range(kc):
                    op("pe", lambda E, c=c, pb=pb, half=half: E.matmul(
                        pb[0:64, :], lhsT=wsw[:, c, 0:64], rhs=src[:, c, half * 512:(half + 1) * 512],
                        start=(c == 0), stop=(c == kc - 1)), R=[wsw_d, src_d], W=[pb_d])
                hs = slice(half * 512, (half + 1) * 512)
                op("dve", lambda E, pa=pa, hs=hs: E.tensor_tensor(out=t1[0:64, :], in0=pa[0:64, :],
                                                                  in1=cosT[0:64, hs], op=ALU.mult),
                   R=[pa_d, cs_d], W=[t1_d])
                op("dve", lambda E, pb=pb, hs=hs: E.tensor_tensor(out=t2[0:64, :], in0=pb[0:64, :],
                                                                  in1=sinT[0:64, hs], op=ALU.mult),
                   R=[pb_d, cs_d], W=[t2_d])
                op("pool", lambda E, hs=hs: E.tensor_tensor(out=dst[0:64, hs], in0=t1[0:64, :], in1=t2[0:64, :],
                                                            op=ALU.add), R=[t1_d, t2_d], W=[dst_d])

        def build_swap(wsw, wsw_d, wt, wd, kc, colA):
            op("pool", lambda E: E.tensor_copy(out=wsw[:, 0:kc, 0:32], in_=wt[:, :, colA + 32:colA + 64]),
               R=[wd], W=[wsw_d])
            op("pool", lambda E: E.tensor_copy(out=wsw[:, 0:kc, 32:64], in_=wt[:, :, colA:colA + 32]),
               R=[wd], W=[wsw_d])

        def P6a(ybuf, y_d, es):
            def al(name, shape, dt):
                return es.enter_context(nc.sbuf_tensor(name, shape, dt))
            junk = al("junk6", [128, D], BF16)
            xn = [al("xn6_%d" % i, [128, D], BF16) for i in range(2)]
            gkv = al("gkv", [128, D], F32)
            g1 = al("g1", [128, D], F32)
            gkv_d, g1_d = Dep(), Dep()
            load_bcast(gkv, kv_norm, gkv_d)
            load_bcast(g1, norm_pre[1], g1_d)

            def get_x(b):
                return ybuf[:, b, :], y_d[b]
            norm_transpose(get_x, [(gkv, gkv_d), (g1, g1_d)], [(B, B_d), (A, A_d)], junk, Dep(), xn, [Dep(), Dep()])

        def P6b(es):
            def al(name, shape, dt):
                return es.enter_context(nc.sbuf_tensor(name, shape, dt))
            ckvT = al("ckvT", [128, 4, T], BF16)
            kst = [al("kst6_%d" % i, [128, T], BF16) for i in range(2)]
            vst = [al("vst6_%d" % i, [128, 512], BF16) for i in range(2)]
            wr = al("wr6", [128, KC, 64], BF16)
            wsw = al("wsw6", [128, KC, 64], BF16)
            cosT = al("cos6", [64, T], F32)
            sinT = al("sin6", [64, T], F32)
            t1 = al("t1_6", [64, 512], F32)
            t2 = al("t2_6", [64, 512], F32)
            krT = al("krT6", [64, T], BF16)
            ckvT_d, wr_d, wsw_d, cs_d = Dep(), Dep(), Dep(), Dep()
            t1_d, t2_d, krT_d = Dep(), Dep(), Dep()
            op("sp", lambda E: E.dma_start(out=cosT[:], in_=cosT_in), W=[cs_d], dma=misc_sem)
            op("sp", lambda E: E.dma_start(out=sinT[:], in_=sinT_in), W=[cs_d], dma=misc_sem)
            op("pool", lambda E: E.dma_start(out=wr[:], in_=w_dkv[:, 512:576].rearrange("(c p) n -> p c n", p=128)),
               W=[wr_d], dma=ld_sem[2])
            build_swap(wsw, wsw_d, wr, wr_d, KC, 0)
            latent_norm_T(es, "kv", w_dkv[:, 0:512], B, B_d, kv_lat_norm, ckvT, ckvT_d)
            rope_fm(wr, wr_d, KC, 0, wsw, wsw_d, B, B_d, cosT, sinT, cs_d, t1, t1_d, t2, t2_d, krT, krT_d)
            kr_in_d = Dep()
            op("sp", lambda E: E.dma_start(out=krin, in_=krT[:]), R=[krT_d], W=[kr_in_d], dma=st_sem[0])
            gather(krin, krall, [kr_in_d], krall_d)
            kst_d = [Dep(), Dep()]
            vst_d = [Dep(), Dep()]
            kin_d = [Dep() for _ in range(H)]
            vin_d = [Dep() for _ in range(32)]
            i = 0
            steps = []
            for cg in range(4):
                steps.append(("k", cg, w_uk[:, cg * 512:(cg + 1) * 512]))
                steps.append(("v", cg, w_uv[:, cg * 512:(cg + 1) * 512]))
            loaded = {0: load_w(steps[0][2], 4, 512)}
            for si, (kind, cg, _) in enumerate(steps):
                wt, wd = loaded.pop(si)
                if si + 1 < len(steps):
                    loaded[si + 1] = load_w(steps[si + 1][2], 4, 512)
                if kind == "k":
                    for hh in range(4):
                        h = cg * 4 + hh
                        k = h % 2
                        proj_fm(wt, wd, 4, hh * 128, 128, ckvT, ckvT_d, kst[k], kst_d[k])
                        op("sp", lambda E, hh=hh, cg=cg, k=k: E.dma_start(
                            out=kin[1][cg][hh * 128:(hh + 1) * 128, :], in_=kst[k][:]),
                           R=[kst_d[k]], W=[kin_d[h]], dma=st_sem[k])
                    gather(kin[1][cg], kall[1][cg], kin_d[cg * 4:cg * 4 + 4], kall_d[1][cg],
                           defer=(1, cg) if cg > 0 else None)
                else:
                    for b in range(NB):
                        k = i % 2
                        i += 1
                        p, p_d = next_ps()
                        for c in range(4):
                            op("pe", lambda E, c=c, p=p, wt=wt, b=b: E.matmul(
                                p[:], lhsT=ckvT[:, c, b * 128:(b + 1) * 128], rhs=wt[:, c, :],
                                start=(c == 0), stop=(c == 3)), R=[wd, ckvT_d], W=[p_d])
                        evac(vst[k][:], p[:], [p_d], [vst_d[k]])
                        op("sp", lambda E, b=b, cg=cg, k=k: E.dma_start(
                            out=vin[1][cg][b * 128:(b + 1) * 128, :], in_=vst[k][:]),
                           R=[vst_d[k]], W=[vin_d[cg * NB + b]], dma=st_sem[2 + k])
                    gather(vin[1][cg], vall[1][cg], vin_d[cg * NB:(cg + 1) * NB], vall_d[1][cg],
                           defer=(1, cg) if cg > 0 else None)
            latent_norm_T(es, "q", w_in_b[:, 0:512], A, A_d, q_lat_norm, cqT, cqT_d)

        def P9():
          with ExitStack() as es:
            def al(name, shape, dt):
                return es.enter_context(nc.sbuf_tensor(name, shape, dt))
            kh = [al("kh9_%d" % i, [128, SEQ], BF16) for i in range(2)]
            vh = [al("vh9_%d" % i, [128, 32, 128], BF16) for i in range(2)]
            kr = al("kr9", [64, SEQ], BF16)
            scb = al("scb", [128, 9, 512], F32)
            qn = [al("qn%d" % i, [128, T], BF16) for i in range(2)]
            qr = [al("qr%d" % i, [64, T], BF16) for i in range(2)]
            gT = [al("gT9_%d" % i, [128, T], BF16) for i in range(2)]
            cosT = al("cos9", [64, T], F32)
            sinT = al("sin9", [64, T], F32)
            mb1 = al("mb1s", [128, 2, 512], F32)
            wbt = [al("wb9_%d" % i, [128, 512], BF16) for i in range(3)]
            wTt = [al("wT9_%d" % i, [128, 512], BF16) for i in range(3)]
            t1 = al("t1_9", [64, 512], F32)
            t2 = al("t2_9", [64, 512], F32)
            wsw = al("wsw9", [128, 4, 64], BF16)
            mxt = [al("mxt%d" % i, [128, 8], F32) for i in range(4)]
            rst = [al("rst%d" % i, [128, 8], F32) for i in range(4)]
            dg = [al("dg%d" % i, [128, 128], F32) for i in range(2)]
            tg = [al("tg%d" % i, [128, 128], F32) for i in range(2)]
            kh_d, vh_d, qn_d, qr_d, gT_d = ([Dep(), Dep()] for _ in range(5))
            wb_d, wT_d, mx_d, rs_d, dg_d, tg_d = ([Dep(), Dep(), Dep(), Dep()] for _ in range(6))
            kr_d, cs_d, mask_d, t1_d, t2_d, wsw_d = (Dep() for _ in range(6))
            sc_d = [Dep() for _ in range(9)]
            op("sp", lambda E: E.dma_start(out=cosT[:], in_=cosT_in), W=[cs_d], dma=misc_sem)
            op("sp", lambda E: E.dma_start(out=sinT[:], in_=sinT_in), W=[cs_d], dma=misc_sem)
            op("sp", lambda E: E.dma_start(out=mb1[:], in_=mb1_in.rearrange("k p n -> p k n")),
               W=[mask_d], dma=misc_sem)
            for r in range(4):
                for par in range(2):
                    nb0 = r if par == 0 else 7 - r
                    src = krall[r * 64:(r + 1) * 64, :].rearrange("p (s two n) -> p two s n", two=2, n=128)[:, par, :, :]
                    dst = kr[:, :].rearrange("p (s e n) -> p e s n", e=8, n=128)[:, nb0, :, :]
                    op("sp", lambda E, src=src, dst=dst: E.dma_start(out=dst, in_=src),
                       R=[krall_d], W=[kr_d], dma=kr_sem)
            hw = {}

            def head_load(h):
                hb = h % 2
                load_kv_head(1, h, kh[hb], kh_d[hb], vh[hb], vh_d[hb], ld_sem[hb])
                hw[h] = (load_w(w_uq[:, h * 192:(h + 1) * 192], 4, 192),
                         load_w(w_in_b[:, 512 + h * 128: 512 + (h + 1) * 128], KC, 128))

            def head_proj(h):
                hb = h % 2
                (wt, wd), (wt2, wd2) = hw.pop(h)
                build_swap(wsw, wsw_d, wt, wd, 4, 128)
                proj_fm(wt, wd, 4, 0, 128, cqT, cqT_d, qn[hb], qn_d[hb])
                rope_fm(wt, wd, 4, 128, wsw, wsw_d, cqT, cqT_d, cosT, sinT, cs_d, t1, t1_d, t2, t2_d,
                        qr[hb], qr_d[hb])
                proj_fm(wt2, wd2, KC, 0, 128, A, A_d, gT[hb], gT_d[hb], func=AF.Silu)

            def ph1(sl, c, r, i):
                h, s, nt, mi = sl["h"], sl["s"], sl["nt"], sl["mi"]
                hb = h % 2
                sl_ = slice(s * 128, (s + 1) * 128)
                ks = slice(c * 512, (c + 1) * 512)
                p, p_d = next_ps(0, 3)
                op("pe", lambda E: E.matmul(p[:], lhsT=qn[hb][:, sl_], rhs=kh[hb][:, ks], start=True, stop=False),
                   R=[qn_d[hb], kh_d[hb]], W=[p_d])
                op("pe", lambda E: E.matmul(p[:], lhsT=qr[hb][0:64, sl_], rhs=kr[0:64, ks], start=False, stop=True),
                   R=[qr_d[hb], kr_d], W=[p_d])
                if c == nt - 1:
                    par = s % 2
                    op("dve", lambda E: E.tensor_tensor(out=scb[:, r, :], in0=p[:], in1=mb1[:, par, :], op=ALU.add),
                       R=[p_d, mask_d], W=[sc_d[r]])
                else:
                    op("act", lambda E: E.activation(out=scb[:, r, :], in_=p[:], func=AF.Copy),
                       R=[p_d], W=[sc_d[r]])
                op("dve", lambda E: E.tensor_reduce(out=mxt[mi][:, c:c + 1], in_=scb[:, r, :], axis=AX.X, op=ALU.max),
                   R=[sc_d[r]], W=[mx_d[mi]])

            def mid(sl):
                nt, mi = sl["nt"], sl["mi"]
                m, m_d = stat_col()
                nb_, nb_d = stat_col()
                op("dve", lambda E: E.tensor_reduce(out=m, in_=mxt[mi][:, 0:nt], axis=AX.X, op=ALU.max),
                   R=[mx_d[mi]], W=[m_d])
                op("dve", lambda E: E.tensor_scalar(out=nb_, in0=m, scalar1=-SC_B, scalar2=None, op0=ALU.mult),
                   R=[m_d], W=[nb_d])
                sl["nb"] = (nb_, nb_d)
                sl["po"] = next_ps(3, 5)

            def stC(sl, c, r, i):
                mi = sl["mi"]
                k3 = i % 3
                nb_, nb_d = sl["nb"]
                op("act", lambda E: E.activation(out=wbt[k3][:], in_=scb[:, r, :], func=AF.Exp, bias=nb_, scale=SC_B,
                                                 accum_out=rst[mi][:, c:c + 1]),
                   R=[sc_d[r], nb_d], W=[wb_d[k3], rs_d[mi]])

            def stD(sl, c, r, i):
                k3 = i % 3
                pv_T(wbt[k3], wb_d[k3], wTt[k3], wT_d[k3])

            def stE(sl, c, r, i):
                h, s, nt, mi = sl["h"], sl["s"], sl["nt"], sl["mi"]
                hb = h % 2
                sb = mi % 2
                sl_ = slice(s * 128, (s + 1) * 128)
                k3 = i % 3
                po, po_d = sl["po"]
                pv_mm(wTt[k3], wT_d[k3], vh[hb], vh_d[hb], po, po_d, c, nt)
                if c == nt - 1:
                    rs, rs1_d = stat_col()
                    rr, rr_d = stat_col()
                    op("dve", lambda E: E.tensor_reduce(out=rs, in_=rst[mi][:, 0:nt], axis=AX.X, op=ALU.add),
                       R=[rs_d[mi]], W=[rs1_d])
                    op("dve", lambda E: E.reciprocal(out=rr, in_=rs), R=[rs1_d], W=[rr_d])
                    op("dve", lambda E: E.tensor_scalar(out=dg[sb][:], in0=identf[:], scalar1=rr, scalar2=None,
                                                        op0=ALU.mult), R=[rr_d, const_d], W=[dg_d[sb]])
                    prb, prb_d = next_ps(5, 6)
                    op("pe", lambda E: E.matmul(prb[:, 0:128], lhsT=onesf[:], rhs=dg[sb][:], start=True, stop=True),
                       R=[dg_d[sb], const_d], W=[prb_d])
                    op("dve", lambda E: E.tensor_tensor(out=tg[sb][:], in0=prb[:, 0:128], in1=gT[hb][:, sl_],
                                                        op=ALU.mult), R=[prb_d, gT_d[hb]], W=[tg_d[sb]])
                    op("dve", lambda E: E.tensor_tensor(out=B[:, h, sl_], in0=po[:, 0:128], in1=tg[sb][:],
                                                        op=ALU.mult), R=[po_d, tg_d[sb]], W=[B_d])

            def run_round(cur, prev):
                ct = cur["tiles"] if cur else []
                pt_ = prev["tiles"] if prev else []
                if prev is not None:
                    for sl in prev["slots"]:
                        mid(sl)
                n = max(len(ct), len(pt_))
                for t in range(-1, n + 2):
                    if 0 <= t + 1 < len(pt_):
                        stC(*pt_[t + 1])
                    if 0 <= t < len(ct):
                        ph1(*ct[t])
                    if 0 <= t < len(pt_):
                        stD(*pt_[t])
                    if 0 <= t - 1 < len(pt_):
                        stE(*pt_[t - 1])

            prev = None
            head_load(0)
            head_proj(0)
            gi = 0
            rnd = 0
            for h in range(H):
                for ri, pair in enumerate(((0, 7), (1, 6), (2, 5), (3, 4))):
                    if h in (0, 2, 4) and ri == 2:
                        emit_deferred(1, 1 + h // 2, 0)
                    if h in (1, 3, 5) and ri == 0:
                        emit_deferred(1, 1 + h // 2, 1)
                    if ri == 1 and h + 1 < H:
                        head_load(h + 1)
                    if ri == 3 and h + 1 < H:
                        head_proj(h + 1)
                    if ri == 0 and h == H - 1:
                        prefetch_w(("out", 1, 0), w_out_b[:, 0:512], KC, 512)
                        prefetch_w(("out", 1, 1), w_out_b[:, 512:1024], KC, 512)
                    slots = []
                    r = 0
                    for si_, s_ in enumerate(pair):
                        slots.append({"h": h, "s": s_, "nt": s_ + 1, "r0": r, "mi": (rnd % 2) * 2 + si_})
                        r += s_ + 1
                    tiles = []
                    for sl in slots:
                        for c in range(sl["nt"]):
                            tiles.append((sl, c, sl["r0"] + c, gi))
                            gi += 1
                    cur = {"slots": slots, "tiles": tiles}
                    run_round(cur, prev)
                    prev = cur
                    rnd += 1
            run_round(None, prev)
            sc.end_phase()

        def dump(src_rows_fn):
            pass

        cqT_d = Dep()
        P01()
        if upto >= 3:
            P3()
        if upto >= 4:
            with ExitStack() as es0:
                ybuf = es0.enter_context(nc.sbuf_tensor("ybuf0", [128, NB, D], F32))
                y_d = [Dep() for _ in range(NB)]
                with ExitStack() as es:
                    P_out(0, w_out_a, ybuf, y_d, x_own, es)
                    sc.end_phase()
                if upto >= 5:
                    with ExitStack() as es:
                        P_ple(0, ybuf, y_d, es)
                        prefetch_w(("lat", "kv"), w_dkv[:, 0:512], KC, 512)
                        xs_d = Dep()
                        store_rows(xs, ybuf, y_d, st_sem[1], xs_d)
                        if debug and upto == 5:
                            store_rows(dbg, ybuf, y_d, st_sem[2])
                        sc.end_phase()
                elif debug:
                    store_rows(dbg, ybuf, y_d, st_sem[2])
                    sc.end_phase()
                if upto >= 6:
                    with ExitStack() as es:
                        P6a(ybuf, y_d, es)
                        sc.end_phase()
        if upto >= 6:
            with ExitStack() as es:
                P6b(es)
                sc.end_phase()
        if upto >= 9:
            P9()
        if upto >= 10:
            with ExitStack() as es0:
                ybuf = es0.enter_context(nc.sbuf_tensor("ybuf1", [128, NB, D], F32))
                y_d = [Dep() for _ in range(NB)]
                with ExitStack() as es:
                    P_out(1, w_out_b, ybuf, y_d, xs, es)
                    sc.end_phase()
                with ExitStack() as es:
                    P_ple(1, ybuf, y_d, es)
                    store_rows(out_own, ybuf, y_d, st_sem[1])
                    sc.end_phase()
        elif not debug or upto < 4:
            pass
        sc.end_phase()
        print("instructions issued:", sc.ninst, {e: sc.prog[e].cnt for e in sc.ENG})
    return nc


def host_prep(inputs):
    x = np.ascontiguousarray(inputs["x"], dtype=np.float32)
    p = np.ascontiguousarray(inputs["p"], dtype=np.float32)
    in_maps = []
    inv = (1.0 / (10000.0 ** (np.arange(0, 64, 2, dtype=np.float32) / np.float32(64)))).astype(np.float32)
    jj = np.arange(512)[None, :]
    ii = np.arange(128)[:, None]
    ident = np.eye(128, dtype=np.float32)
    shared = {}
    for k in ("norm_pre", "norm_post", "kv_norm", "w_dkv", "kv_latent_norm", "w_uk", "w_uv",
              "w_ple_proj", "w_ple_gate"):
        shared[k] = np.ascontiguousarray(inputs[k], dtype=np.float32)
    for k in ("w_in_a", "w_out_a", "w_in_b", "q_latent_norm", "w_uq", "w_out_b"):
        shared[k] = np.ascontiguousarray(inputs[k][0], dtype=np.float32)
    shared["ident"] = ident
    toks = []
    for c in range(8):
        b, r = c // 4, c % 4
        blocks = [qblock(r, s) for s in range(NB)]
        tok = np.concatenate([np.arange(q * 128, (q + 1) * 128) for q in blocks])
        toks.append((b, tok))
        m = dict(shared)
        m["x_own"] = np.ascontiguousarray(x[b, tok, :])
        m["pT"] = np.ascontiguousarray(np.transpose(p[:, b, tok, :], (0, 2, 1)))
        m01 = np.zeros((2, 128, 512), np.float32)
        mb1 = np.zeros((2, 128, 512), np.float32)
        for par in range(2):
            o = r if par == 0 else 3 - r
            m01[par] = (jj < o * 128 + ii).astype(np.float32)
            mb1[par] = np.where((jj // 64) <= ((o * 128 + ii) // 64), 0.0, -1.0e5).astype(np.float32)
        m["m01"] = m01
        m["mneg"] = ((m01 - 1.0) * 30000.0).astype(np.float32)
        m["mb1"] = mb1
        inv64 = 1.0 / (10000.0 ** (np.arange(0, 64, 2, dtype=np.float64) / 64.0))
        ang = tok.astype(np.float64)[:, None] * inv64[None, :]
        cos = np.cos(ang).astype(np.float32).T
        sin = np.sin(ang).astype(np.float32).T
        m["cosT"] = np.ascontiguousarray(np.concatenate([cos, cos], 0))
        m["sinT"] = np.ascontiguousarray(np.concatenate([-sin, sin], 0))
        in_maps.append(m)
    return in_maps, toks


_CACHE = {}


def kernel(**inputs):
    in_maps, toks = host_prep(inputs)
    if "nc" not in _CACHE:
        _CACHE["nc"] = build_program()
    nc = _CACHE["nc"]
    res = run_bass_kernel_spmd(nc, in_maps, core_ids=list(range(8)))
    out = np.empty((2, SEQ, D), np.float32)
    for c in range(8):
        b, tok = toks[c]
        out[b, tok, :] = res.results[c]["out_own"]
    return out
```

```python
import os
import numpy as np
from contextlib import ExitStack
import concourse.bass as bass
import concourse.mybir as mybir
from concourse.bass_utils import run_bass_kernel_spmd

F32 = mybir.dt.float32
BF16 = mybir.dt.bfloat16
AF = mybir.ActivationFunctionType
ALU = mybir.AluOpType
AX = mybir.AxisListType

D = 2048
SEQ = 4096
T = 1024
NB = 8
KC = 16
H = 16
EPS = 1e-6
NEG = -30000.0
GROUPS = [[0, 1, 2, 3], [4, 5, 6, 7]]
SC_A = 128 ** -0.5
SC_B = 192 ** -0.5


def qblock(r, s):
    return 8 * (s // 2) + (r if s % 2 == 0 else 7 - r)


class Dep:
    __slots__ = ("w", "r")

    def __init__(self):
        self.w = None
        self.r = {}


class SemObj:
    def __init__(self, h):
        self.h = h
        self.cnt = 0
        self.deps = []


class Sched:
    ENG = ("pe", "act", "dve", "pool", "sp")

    def __init__(self, nc, block):
        self.nc = nc
        self.block = block
        self.prog = {e: SemObj(nc.alloc_semaphore("prog_" + e)) for e in self.ENG}
        self.q = {e: [] for e in self.ENG}
        self.seen = {e: {} for e in self.ENG}
        self.dsems = []
        self.rr = 0
        self.ninst = 0
        self.stop_at = int(os.environ.get("KSTOP", "1000000000"))
        self.trace = False
        self.names = []

    def dsem(self, name):
        s = SemObj(self.nc.alloc_semaphore(name))
        self.dsems.append(s)
        return s

    def _wait(self, eng, sem, val):
        if val <= 0:
            return
        seen = self.seen[eng]
        if seen.get(sem, 0) >= val:
            return
        seen[sem] = val
        h = sem.h
        self.q[eng].append(lambda E, h=h, val=val: E.wait_ge(h, val))

    def op(self, eng, fn, R=(), W=(), dma=None, inc=None):
        own = self.prog[eng]
        if self.ninst >= self.stop_at:
            return
        self.ninst += 1
        if self.trace:
            self.names.append((self.ninst, eng))
        for d in R:
            if d.w is not None:
                self._wait(eng, d.w[0], d.w[1])
        strict = eng != "pe"
        for d in W:
            if d.w is not None and (strict or d.w[0] is not own):
                self._wait(eng, d.w[0], d.w[1])
            for sem, val in d.r.items():
                if strict or sem is not own:
                    self._wait(eng, sem, val)
        if dma is not None:
            dma.cnt += 16
            tok = (dma, dma.cnt)
            h = dma.h
            self.q[eng].append(lambda E, fn=fn, h=h: fn(E).then_inc(h, 16))
        elif inc is not None:
            inc.cnt += 1
            tok = (inc, inc.cnt)
            h = inc.h
            self.q[eng].append(lambda E, fn=fn, h=h: fn(E).then_inc(h, 1))
        else:
            own.cnt += 1
            tok = (own, own.cnt)
            h = own.h
            self.q[eng].append(lambda E, fn=fn, h=h: fn(E).then_inc(h, 1))
        for d in R:
            if d.r.get(tok[0], 0) < tok[1]:
                d.r[tok[0]] = tok[1]
        for d in W:
            d.w = tok
            d.r = {}
        if dma is not None:
            keep = []
            for d in dma.deps:
                if d.w is not None and d.w[0] is dma:
                    d.w = tok
                    keep.append(d)
            for d in W:
                if d not in keep:
                    keep.append(d)
            dma.deps = keep

    def wait_for_write(self, eng, W):
        for d in W:
            if d.w is not None:
                self._wait(eng, d.w[0], d.w[1])
            for sem, val in d.r.items():
                self._wait(eng, sem, val)

    def set_writer(self, W, sem):
        tok = (sem, sem.cnt)
        for d in W:
            d.w = tok
            d.r = {}
            if d not in sem.deps:
                sem.deps.append(d)

    def barrier(self):
        for e in self.ENG:
            for e2 in self.ENG:
                self._wait(e, self.prog[e2], self.prog[e2].cnt)
            for s in self.dsems:
                self._wait(e, s, s.cnt)

    def flush(self):
        m = {"pe": self.block.tensor, "act": self.block.scalar, "dve": self.block.vector,
             "pool": self.block.gpsimd, "sp": self.block.sync}
        for e in self.ENG:
            lst = self.q[e]
            if not lst:
                continue
            self.q[e] = []

            def body(E, lst=lst):
                for f in lst:
                    f(E)
            m[e](body)

    def end_phase(self):
        self.barrier()
        self.flush()

    def alt(self):
        self.rr ^= 1
        return "act" if self.rr else "dve"


def build_program(upto=99, debug=False):
    nc = bass.Bass("TRN2", target_bir_lowering=False)

    def din(name, shape, dt=F32):
        return nc.dram_tensor(name, list(shape), dt, kind="ExternalInput").ap()

    def dint(name, shape, dt=BF16):
        return nc.dram_tensor(name, list(shape), dt, kind="Internal").ap()

    x_own = din("x_own", [T, D])
    pT_in = din("pT", [2, 256, T])
    norm_pre = din("norm_pre", [2, D])
    norm_post = din("norm_post", [2, D])
    w_in_a = din("w_in_a", [D, 4 * D])
    w_out_a = din("w_out_a", [D, D])
    w_in_b = din("w_in_b", [D, 2560])
    q_lat_norm = din("q_latent_norm", [512])
    w_uq = din("w_uq", [512, 3072])
    w_out_b = din("w_out_b", [D, D])
    kv_norm = din("kv_norm", [D])
    w_dkv = din("w_dkv", [D, 576])
    kv_lat_norm = din("kv_latent_norm", [512])
    w_uk = din("w_uk", [512, D])
    w_uv = din("w_uv", [512, D])
    w_pp = din("w_ple_proj", [2, 256, D])
    w_pg = din("w_ple_gate", [2, D, D])
    m01_in = din("m01", [2, 128, 512])
    mneg_in = din("mneg", [2, 128, 512])
    mb1_in = din("mb1", [2, 128, 512])
    cosT_in = din("cosT", [64, T])
    sinT_in = din("sinT", [64, T])
    ident_in = din("ident", [128, 128])
    out_own = nc.dram_tensor("out_own", [T, D], F32, kind="ExternalOutput").ap()
    dbg = nc.dram_tensor("dbg", [T, D], F32, kind="ExternalOutput").ap() if debug else None

    kin = [[dint("kin%d_%d" % (l, g), [512, T]) for g in range(4)] for l in range(2)]
    kall = [[dint("kall%d_%d" % (l, g), [4 * 512, T]) for g in range(4)] for l in range(2)]
    vin = [[dint("vin%d_%d" % (l, g), [T, 512]) for g in range(4)] for l in range(2)]
    vall = [[dint("vall%d_%d" % (l, g), [4 * T, 512]) for g in range(4)] for l in range(2)]
    krin = dint("krin", [64, T])
    krall = dint("krall", [4 * 64, T])
    xs = dint("xs", [T, D], F32)

    A = nc.alloc_sbuf_tensor("bufA", [128, KC, T], BF16)
    B = nc.alloc_sbuf_tensor("bufB", [128, KC, T], BF16)
    A_d, B_d = Dep(), Dep()
    WN = 8192
    wbuf = [nc.alloc_sbuf_tensor("wbuf%d" % i, [128, WN], BF16) for i in range(2)]
    identb = nc.alloc_sbuf_tensor("identb", [128, 128], BF16)
    identf = nc.alloc_sbuf_tensor("identf", [128, 128], F32)
    onesf = nc.alloc_sbuf_tensor("onesf", [128, 128], F32)
    stats = nc.alloc_sbuf_tensor("stats", [128, 512], F32)
    cqT = nc.alloc_sbuf_tensor("cqT", [128, 4, T], BF16)
    ps = [nc.alloc_psum_tensor("ps%d" % i, [128, 512], F32) for i in range(6)]
    psb = [nc.alloc_psum_tensor("psb%d" % i, [128, 1024], BF16) for i in range(2)]

    with nc.Block() as block:
        sc = Sched(nc, block)
        op = sc.op
        ps_d = [Dep() for _ in ps]
        psb_d = [Dep() for _ in psb]
        wbuf_d = [Dep() for _ in wbuf]
        wsem = [sc.dsem("wsem%d" % i) for i in range(2)]
        ld_sem = [sc.dsem("ld%d" % i) for i in range(4)]
        st_sem = [sc.dsem("st%d" % i) for i in range(4)]
        misc_sem = sc.dsem("misc")
        pm_sem = sc.dsem("pmisc")
        kr_sem = sc.dsem("krld")
        whw_sem = sc.dsem("whw")
        cc_sem = SemObj(nc.alloc_semaphore("ccsem"))
        const_d = Dep()
        state = {"w": 0, "ps": 0, "psb": 0, "stat": 0}

        def next_ps(lo=0, hi=6):
            key = ("ps", lo, hi)
            i = state.get(key, lo)
            state[key] = lo + (i + 1 - lo) % (hi - lo)
            return ps[i], ps_d[i]

        def next_psb():
            i = state["psb"]
            state["psb"] = (i + 1) % len(psb)
            return psb[i], psb_d[i]

        stat_deps = [Dep() for _ in range(512)]

        def stat_col():
            i = state["stat"]
            state["stat"] = (i + 1) % 512
            return stats[:, i:i + 1], stat_deps[i]

        prefetched = {}

        def prefetch_w(key, src2d, kc, n):
            prefetched[key] = load_w(src2d, kc, n)

        def load_w(src2d, kc, n, key=None):
            if key is not None and key in prefetched:
                return prefetched.pop(key)
            i = state["w"]
            state["w"] = 1 - i
            dst = wbuf[i][:, 0:kc * n].rearrange("p (c n) -> p c n", n=n)
            src = src2d.rearrange("(c p) n -> p c n", p=128)
            stg = state.get("stg")
            if stg is not None and i == 1 and kc * n == WN:
                sg, sg_d = stg
                op("sp", lambda E: E.dma_start(out=sg[:, 0:kc * n].rearrange("p (c n) -> p c n", n=n), in_=src),
                   W=[sg_d], dma=whw_sem)
                hf = kc * n // 2
                op("act", lambda E: E.activation(out=wbuf[i][:, 0:hf], in_=sg[:, 0:hf], func=AF.Copy),
                   R=[sg_d], W=[wbuf_d[i]])
                op("dve", lambda E: E.tensor_copy(out=wbuf[i][:, hf:2 * hf], in_=sg[:, hf:2 * hf]),
                   R=[sg_d], W=[wbuf_d[i]])
                return dst, wbuf_d[i]
            op("pool", lambda E: E.dma_start(out=dst, in_=src), W=[wbuf_d[i]], dma=wsem[i])
            return dst, wbuf_d[i]

        def evac(out, in_, R, W, eng=None):
            e = eng or sc.alt()
            if e == "act":
                op(e, lambda E: E.activation(out=out, in_=in_, func=AF.Copy), R=R, W=W)
            else:
                op(e, lambda E: E.tensor_copy(out=out, in_=in_), R=R, W=W)

        def load_bcast(dst, vec_ap, d):
            op("sp", lambda E: E.dma_start(out=dst[:], in_=vec_ap.partition_broadcast(128)),
               W=[d], dma=misc_sem)

        def rstd_from_ssq(ssq, ssq_d, n):
            a, a_d = stat_col()
            b, b_d = stat_col()
            c, c_d = stat_col()
            op("dve", lambda E: E.tensor_scalar(out=a, in0=ssq, scalar1=1.0 / n, scalar2=EPS,
                                                op0=ALU.mult, op1=ALU.add), R=[ssq_d], W=[a_d])
            op("act", lambda E: E.activation(out=b, in_=a, func=AF.Sqrt), R=[a_d], W=[b_d])
            op("dve", lambda E: E.reciprocal(out=c, in_=b), R=[b_d], W=[c_d])
            return c, c_d

        def transpose_block(src, src_d, ncols, dst, dst_d, b):
            nch = ncols // 128
            for g0 in range(0, nch, 4):
                gn = min(4, nch - g0)
                pt, pt_d = next_psb()
                for j in range(gn):
                    c = g0 + j
                    op("pe", lambda E, c=c, j=j, pt=pt: E.transpose(
                        out=pt[:, j * 128:(j + 1) * 128], in_=src[:, c * 128:(c + 1) * 128],
                        identity=identb[:]), R=[src_d, const_d], W=[pt_d])
                o = dst[:, g0:g0 + gn, b * 128:(b + 1) * 128]
                i_ = pt[:, 0:gn * 128].rearrange("p (c n) -> p c n", n=128)
                evac(o, i_, [pt_d], [dst_d])

        def norm_transpose(get_x, gains, outs, junk, junk_d, xn, xn_d):
            for b in range(NB):
                xa, xa_d = get_x(b)
                ssq, ssq_d = stat_col()
                op("act", lambda E, xa=xa, ssq=ssq: E.activation(out=junk[:], in_=xa, func=AF.Square,
                                                                 accum_out=ssq),
                   R=[xa_d], W=[junk_d, ssq_d])
                r, r_d = rstd_from_ssq(ssq, ssq_d, D)
                for gi, ((gb, gb_d), (dst, dst_d)) in enumerate(zip(gains, outs)):
                    k = (b * len(gains) + gi) % 2
                    op("dve", lambda E, xa=xa, r=r, gb=gb, k=k: E.scalar_tensor_tensor(
                        out=xn[k][:], in0=xa, scalar=r, in1=gb[:], op0=ALU.mult, op1=ALU.mult),
                       R=[xa_d, r_d, gb_d], W=[xn_d[k]])
                    transpose_block(xn[k], xn_d[k], D, dst, dst_d, b)

        deferred = {0: {}, 1: {}}

        def gather(src, dst, src_ds, dst_d, defer=None):
            if defer is not None:
                deferred[defer[0]].setdefault(defer[1], []).append((src, dst, list(src_ds), dst_d))
                return
            op("pool", lambda E: E.collective_compute("AllGather", ALU.bypass, replica_groups=GROUPS,
                                                      ins=[src], outs=[dst]), R=src_ds, W=[dst_d], inc=cc_sem)

        def emit_deferred(l, cg, which):
            lst = deferred[l].get(cg, [])
            if which < len(lst) and lst[which] is not None:
                src, dst, src_ds, dst_d = lst[which]
                lst[which] = None
                gather(src, dst, src_ds, dst_d)

        def proj_fm(wt, wd, kc, col0, m, src, src_d, dst, dst_d, func=None):
            for half in range(2):
                p, p_d = next_ps()
                for c in range(kc):
                    op("pe", lambda E, c=c, p=p, half=half: E.matmul(
                        p[0:m, :], lhsT=wt[:, c, col0:col0 + m], rhs=src[:, c, half * 512:(half + 1) * 512],
                        start=(c == 0), stop=(c == kc - 1)), R=[wd, src_d], W=[p_d])
                o = dst[0:m, half * 512:(half + 1) * 512]
                if func is None:
                    evac(o, p[0:m, :], [p_d], [dst_d])
                else:
                    op("act", lambda E, o=o, p=p: E.activation(out=o, in_=p[0:m, :], func=func),
                       R=[p_d], W=[dst_d])

        op("pool", lambda E: E.dma_start(out=identb[:], in_=ident_in), W=[const_d], dma=pm_sem)
        op("sp", lambda E: E.dma_start(out=identf[:], in_=ident_in), W=[const_d], dma=misc_sem)
        op("pool", lambda E: E.memset(onesf[:], 1.0), W=[const_d])
        sc.end_phase()

        kall_d = [[Dep() for _ in range(4)] for _ in range(2)]
        vall_d = [[Dep() for _ in range(4)] for _ in range(2)]
        krall_d = Dep()

        def P01():
          with ExitStack() as es:
            def al(name, shape, dt):
                return es.enter_context(nc.sbuf_tensor(name, shape, dt))
            xt = [al("xt%d" % i, [128, D], F32) for i in range(2)]
            junk = al("junk", [128, D], BF16)
            xn = [al("xn%d" % i, [128, D], BF16) for i in range(2)]
            gb0 = al("gb0", [128, D], F32)
            kst = [al("kst%d" % i, [128, T], BF16) for i in range(2)]
            vst = [al("vst%d" % i, [128, 512], BF16) for i in range(2)]
            stg01 = al("stg01", [128, WN], F32)
            state["stg"] = (stg01[:, :], Dep())
            xt_d = [Dep(), Dep()]
            gb0_d = Dep()
            load_bcast(gb0, norm_pre[0], gb0_d)

            def get_x0(b):
                k = b % 2
                op("sp", lambda E: E.dma_start(out=xt[k][:], in_=x_own[b * 128:(b + 1) * 128, :]),
                   W=[xt_d[k]], dma=ld_sem[k])
                return xt[k][:], xt_d[k]

            norm_transpose(get_x0, [(gb0, gb0_d)], [(A, A_d)], junk, Dep(), xn, [Dep(), Dep()])
            kst_d = [Dep(), Dep()]
            vst_d = [Dep(), Dep()]
            kin_d = [Dep() for _ in range(H)]
            vin_d = [Dep() for _ in range(32)]
            i = 0
            steps = []
            for cg in range(4):
                steps.append(("k", cg, w_in_a[:, D + cg * 512: D + (cg + 1) * 512]))
                steps.append(("v", cg, w_in_a[:, 2 * D + cg * 512: 2 * D + (cg + 1) * 512]))
            loaded = {0: load_w(steps[0][2], KC, 512)}
            for si, (kind, cg, _) in enumerate(steps):
                wt, wd = loaded.pop(si)
                if si + 1 < len(steps):
                    loaded[si + 1] = load_w(steps[si + 1][2], KC, 512)
                if kind == "k":
                    for hh in range(4):
                        h = cg * 4 + hh
                        k = h % 2
                        proj_fm(wt, wd, KC, hh * 128, 128, A, A_d, kst[k], kst_d[k])
                        op("sp", lambda E, hh=hh, cg=cg, k=k: E.dma_start(
                            out=kin[0][cg][hh * 128:(hh + 1) * 128, :], in_=kst[k][:]),
                           R=[kst_d[k]], W=[kin_d[h]], dma=st_sem[k])
                    gather(kin[0][cg], kall[0][cg], kin_d[cg * 4:cg * 4 + 4], kall_d[0][cg],
                           defer=(0, cg) if cg > 0 else None)
                else:
                    for b in range(NB):
                        k = i % 2
                        i += 1
                        p, p_d = next_ps()
                        for c in range(KC):
                            op("pe", lambda E, c=c, p=p, wt=wt, b=b: E.matmul(
                                p[:], lhsT=A[:, c, b * 128:(b + 1) * 128], rhs=wt[:, c, :],
                                start=(c == 0), stop=(c == KC - 1)), R=[wd, A_d], W=[p_d])
                        evac(vst[k][:], p[:], [p_d], [vst_d[k]])
                        op("sp", lambda E, b=b, cg=cg, k=k: E.dma_start(
                            out=vin[0][cg][b * 128:(b + 1) * 128, :], in_=vst[k][:]),
                           R=[vst_d[k]], W=[vin_d[cg * NB + b]], dma=st_sem[2 + k])
                    gather(vin[0][cg], vall[0][cg], vin_d[cg * NB:(cg + 1) * NB], vall_d[0][cg],
                           defer=(0, cg) if cg > 0 else None)
            state["stg"] = None
            sc.end_phase()

        def load_kv_head(l, h, kh, kh_d, vh, vh_d, sem):
            sc.wait_for_write("sp", [kh_d, vh_d])
            for r in range(4):
                for par in range(2):
                    nb0 = r if par == 0 else 7 - r
                    src = kall[l][h // 4][(r * 4 + h % 4) * 128:(r * 4 + h % 4 + 1) * 128, :].rearrange(
                        "p (s two n) -> p two s n", two=2, n=128)[:, par, :, :]
                    dst = kh[:, :].rearrange("p (s e n) -> p e s n", e=8, n=128)[:, nb0, :, :]
                    op("sp", lambda E, src=src, dst=dst: E.dma_start(out=dst, in_=src),
                       R=[kall_d[l][h // 4]], W=[], dma=sem)
                    srcv = vall[l][h // 4][r * T:(r + 1) * T, (h % 4) * 128:(h % 4 + 1) * 128].rearrange(
                        "(s two p) n -> p two s n", two=2, p=128)[:, par, :, :]
                    dstv = vh[:, :, :].rearrange("p (s e) n -> p e s n", e=8)[:, nb0, :, :]
                    op("sp", lambda E, srcv=srcv, dstv=dstv: E.dma_start(out=dstv, in_=srcv),
                       R=[vall_d[l][h // 4]], W=[], dma=sem)
            sc.set_writer([kh_d, vh_d], sem)

        def pv_T(wbt_k, wb_dk, wTt_k, wT_dk):
            pt, pt_d = next_psb()
            for j in range(4):
                op("pe", lambda E, j=j, pt=pt: E.transpose(
                    out=pt[:, j * 128:(j + 1) * 128], in_=wbt_k[:, j * 128:(j + 1) * 128],
                    identity=identb[:]), R=[wb_dk, const_d], W=[pt_d])
            evac(wTt_k[:], pt[:, 0:512], [pt_d], [wT_dk], eng="dve")

        def pv_mm(wTt_k, wT_dk, vh_h, vh_dh, po, po_d, c, nt):
            for j in range(4):
                kb = c * 4 + j
                op("pe", lambda E, j=j, kb=kb: E.matmul(
                    po[:, 0:128], lhsT=vh_h[:, kb, :], rhs=wTt_k[:, j * 128:(j + 1) * 128],
                    start=(kb == 0), stop=(kb == 4 * nt - 1)),
                   R=[vh_dh, wT_dk], W=[po_d])

        def P3():
          with ExitStack() as es:
            def al(name, shape, dt):
                return es.enter_context(nc.sbuf_tensor(name, shape, dt))
            kh = [al("kh%d" % i, [128, SEQ], BF16) for i in range(2)]
            vh = [al("vh%d" % i, [128, 32, 128], BF16) for i in range(2)]
            qT = [al("qT%d" % i, [128, T], BF16) for i in range(2)]
            gT = [al("gT%d" % i, [128, T], BF16) for i in range(2)]
            eb = al("eb", [128, 9, 512], F32)
            Pb = al("Pb", [128, 9 * 513 + 8], F32)
            spt = [al("spt%d" % i, [128, 520], F32) for i in range(2)]
            ut = [al("ut%d" % i, [128, 512], F32) for i in range(2)]
            wbt = [al("wb%d" % i, [128, 512], BF16) for i in range(3)]
            wTt = [al("wT%d" % i, [128, 512], BF16) for i in range(3)]
            m01 = al("m01s", [128, 2, 512], F32)
            ones513 = al("ones513", [128, 520], F32)
            kh_d, vh_d, qT_d, gT_d = ([Dep(), Dep()] for _ in range(4))
            spt_d, ut_d, wb_d, wT_d = ([Dep(), Dep(), Dep()] for _ in range(4))
            e_d = [Dep() for _ in range(9)]
            P_d = [Dep() for _ in range(10)]
            mask_d = Dep()
            op("sp", lambda E: E.dma_start(out=m01[:], in_=m01_in.rearrange("k p n -> p k n")),
               W=[mask_d], dma=misc_sem)
            op("pool", lambda E: E.memset(ones513[:], 1.0), W=[const_d])
            for k in range(2):
                op("pool", lambda E, k=k: E.memset(spt[k][:, 512:520], 0.0), W=[spt_d[k]])
            cnt = {"a": 0, "b": 0}

            hw = {}

            def head_load(h):
                hb = h % 2
                load_kv_head(0, h, kh[hb], kh_d[hb], vh[hb], vh_d[hb], ld_sem[hb])
                hw[h] = (load_w(w_in_a[:, h * 128:(h + 1) * 128], KC, 128),
                         load_w(w_in_a[:, 3 * D + h * 128: 3 * D + (h + 1) * 128], KC, 128))

            def head_proj(h):
                hb = h % 2
                (wt, wd), (wt2, wd2) = hw.pop(h)
                proj_fm(wt, wd, KC, 0, 128, A, A_d, qT[hb], qT_d[hb])
                proj_fm(wt2, wd2, KC, 0, 128, A, A_d, gT[hb], gT_d[hb], func=AF.Silu)

            def ph1a(sl, c, r):
                h, s, nt = sl["h"], sl["s"], sl["nt"]
                hb = h % 2
                p, p_d = next_ps(0, 4)
                qs = qT[hb][:, s * 128:(s + 1) * 128]
                op("pe", lambda E: E.matmul(p[:], lhsT=qs, rhs=kh[hb][:, c * 512:(c + 1) * 512],
                                            start=True, stop=True), R=[qT_d[hb], kh_d[hb]], W=[p_d])
                op("act", lambda E: E.activation(out=eb[:, r, :], in_=p[:], func=AF.Exp, scale=SC_A),
                   R=[p_d], W=[e_d[r]])
                if c == nt - 1:
                    par = s % 2
                    op("pool", lambda E: E.tensor_tensor(out=eb[:, r, :], in0=eb[:, r, :], in1=m01[:, par, :],
                                                         op=ALU.mult), R=[e_d[r], mask_d], W=[e_d[r]])
                if c == 0:
                    op("pool", lambda E: E.memset(Pb[:, 513 * r:513 * r + 1], 0.0), W=[P_d[r]])

            def ph1b(sl, c, r):
                nt = sl["nt"]
                k = cnt["a"] % 2
                cnt["a"] += 1
                op("act", lambda E: E.activation(out=spt[k][:, 0:512], in_=eb[:, r, :], func=AF.Ln, bias=1.0),
                   R=[e_d[r]], W=[spt_d[k]])
                b0 = 513 * r
                n = 512 if c == nt - 1 else 513
                init = 0.0 if c == 0 else Pb[:, b0:b0 + 1]
                wr = [P_d[r]] + ([P_d[r + 1]] if n == 513 else [])
                op("dve", lambda E: E.tensor_tensor_scan(
                    out=Pb[:, b0 + 1:b0 + 1 + n], data0=ones513[:, 0:n], data1=spt[k][:, 0:n],
                    initial=init, op0=ALU.mult, op1=ALU.add),
                   R=[spt_d[k], const_d, P_d[r]], W=wr)

            def mid(sl):
                rl = sl["r0"] + sl["nt"] - 1
                nP, nP_d = stat_col()
                op("dve", lambda E: E.tensor_scalar(out=nP, in0=Pb[:, 513 * rl + 512:513 * rl + 513], scalar1=-1.0,
                                                    scalar2=None, op0=ALU.mult), R=[P_d[rl]], W=[nP_d])
                sl["nP"] = (nP, nP_d)
                sl["po"] = next_ps(4, 6)

            def stC_act(sl, c, r, i):
                k = i % 2
                nP, nP_d = sl["nP"]
                op("act", lambda E: E.activation(out=ut[k][:], in_=Pb[:, 513 * r:513 * r + 512], func=AF.Exp,
                                                 bias=nP, scale=1.0), R=[P_d[r], nP_d], W=[ut_d[k]])

            def stC_pool(sl, c, r, i):
                k = i % 2
                k3 = i % 3
                op("pool", lambda E: E.tensor_tensor(out=wbt[k3][:], in0=eb[:, r, :], in1=ut[k][:], op=ALU.mult),
                   R=[e_d[r], ut_d[k]], W=[wb_d[k3]])

            def stD(sl, c, r, i):
                k3 = i % 3
                pv_T(wbt[k3], wb_d[k3], wTt[k3], wT_d[k3])

            def stE(sl, c, r, i):
                h, s, nt = sl["h"], sl["s"], sl["nt"]
                hb = h % 2
                k3 = i % 3
                po, po_d = sl["po"]
                pv_mm(wTt[k3], wT_d[k3], vh[hb], vh_d[hb], po, po_d, c, nt)
                if c == nt - 1:
                    op("dve", lambda E: E.tensor_tensor(
                        out=B[:, h, s * 128:(s + 1) * 128], in0=po[:, 0:128],
                        in1=gT[hb][:, s * 128:(s + 1) * 128], op=ALU.mult), R=[po_d, gT_d[hb]], W=[B_d])

            def run_round(cur, prev):
                ct = cur["tiles"] if cur else []
                pt_ = prev["tiles"] if prev else []
                if prev is not None:
                    for sl in prev["slots"]:
                        mid(sl)
                n = max(len(ct), len(pt_))
                for t in range(-1, n + 2):
                    if 0 <= t + 1 < len(pt_):
                        stC_act(*pt_[t + 1])
                    if 0 <= t < len(ct):
                        ph1a(*ct[t])
                    if 0 <= t + 1 < len(pt_):
                        stC_pool(*pt_[t + 1])
                    if 0 <= t < len(pt_):
                        stD(*pt_[t])
                    if 0 <= t - 1 < len(pt_):
                        stE(*pt_[t - 1])
                    if 0 <= t - 1 < len(ct):
                        ph1b(*ct[t - 1])

            def run_round_mixed(cur1, cur, prev):
                run_round(cur1, prev)

            prev = None
            head_load(0)
            head_proj(0)
            gi = 0
            for h in range(H):
                for ri, pair in enumerate(((0, 7), (1, 6), (2, 5), (3, 4))):
                    if h in (0, 2, 4) and ri == 2:
                        emit_deferred(0, 1 + h // 2, 0)
                    if h in (1, 3, 5) and ri == 0:
                        emit_deferred(0, 1 + h // 2, 1)
                    if ri == 1 and h + 1 < H:
                        head_load(h + 1)
                    if ri == 3 and h + 1 < H:
                        head_proj(h + 1)
                    if ri == 0 and h == H - 1:
                        prefetch_w(("out", 0, 0), w_out_a[:, 0:512], KC, 512)
                        prefetch_w(("out", 0, 1), w_out_a[:, 512:1024], KC, 512)
                    slots = []
                    r = 0
                    for s_ in pair:
                        slots.append({"h": h, "s": s_, "nt": s_ + 1, "r0": r})
                        r += s_ + 1
                    tiles = []
                    for sl in slots:
                        for c in range(sl["nt"]):
                            tiles.append((sl, c, sl["r0"] + c))
                    cur = {"slots": slots, "tiles": [(sl, c, r_, gi + i) for i, (sl, c, r_) in enumerate(tiles)]}
                    gi += len(tiles)
                    cur1 = {"slots": slots, "tiles": [(sl, c, r_) for (sl, c, r_, _) in cur["tiles"]]}
                    run_round_mixed(cur1, cur, prev)
                    prev = cur
            run_round_mixed(None, None, prev)
            sc.end_phase()

        def P_out(l, w_out, ybuf, y_d, x_src, es):
            def al(name, shape, dt):
                return es.enter_context(nc.sbuf_tensor(name, shape, dt))
            xt = [al("xo%d_%d" % (l, i), [128, D], F32) for i in range(2)]
            junk = al("junko%d" % l, [128, D], BF16)
            gp = al("gpost%d" % l, [128, D], F32)
            xt_d = [Dep(), Dep()]
            junk_d = Dep()
            gp_d = Dep()
            load_bcast(gp, norm_post[l], gp_d)
            state["stg"] = (A.bitcast(F32)[:, :, :].rearrange("p c n -> p (c n)"), A_d)
            def post(b):
                k = b % 2
                op("sp", lambda E, b=b, k=k: E.dma_start(out=xt[k][:], in_=x_src[b * 128:(b + 1) * 128, :]),
                   W=[xt_d[k]], dma=ld_sem[k])
                ssq, ssq_d = stat_col()
                op("act", lambda E, b=b, ssq=ssq: E.activation(out=junk[:], in_=ybuf[:, b, :], func=AF.Square,
                                                               accum_out=ssq), R=[y_d[b]], W=[junk_d, ssq_d])
                r, r_d = rstd_from_ssq(ssq, ssq_d, D)
                op("dve", lambda E, b=b, r=r: E.scalar_tensor_tensor(
                    out=ybuf[:, b, :], in0=ybuf[:, b, :], scalar=r, in1=gp[:], op0=ALU.mult, op1=ALU.mult),
                   R=[y_d[b], r_d, gp_d], W=[y_d[b]])
                op("pool", lambda E, b=b, k=k: E.tensor_tensor(
                    out=ybuf[:, b, :], in0=ybuf[:, b, :], in1=xt[k][:], op=ALU.add),
                   R=[y_d[b], xt_d[k]], W=[y_d[b]])

            for n in range(4):
                wt, wd = load_w(w_out[:, n * 512:(n + 1) * 512], KC, 512, key=("out", l, n))
                for b in range(NB):
                    p, p_d = next_ps()
                    for c in range(KC):
                        op("pe", lambda E, c=c, p=p, wt=wt, b=b: E.matmul(
                            p[:], lhsT=B[:, c, b * 128:(b + 1) * 128], rhs=wt[:, c, :],
                            start=(c == 0), stop=(c == KC - 1)), R=[wd, B_d], W=[p_d])
                    evac(ybuf[:, b, n * 512:(n + 1) * 512], p[:], [p_d], [y_d[b]])
                    if n == 3:
                        post(b)
            prefetch_w(("pg", l, 0), w_pg[l][:, 0:512], KC, 512)
            state["stg"] = None

        def P_ple(l, ybuf, y_d, es):
            def al(name, shape, dt):
                return es.enter_context(nc.sbuf_tensor(name, shape, dt))
            xb16 = [al("xb16_%d_%d" % (l, i), [128, D], BF16) for i in range(2)]
            sgt = [al("sgt%d_%d" % (l, i), [128, 512], F32) for i in range(2)]
            tt = [al("tt%d_%d" % (l, i), [128, 512], F32) for i in range(2)]
            pTt = al("pTt%d" % l, [128, 2, T], BF16)
            wppt = [al("wppt%d_%d" % (l, i), [128, 2, 512], BF16) for i in range(2)]
            xb_d = [Dep(), Dep()]
            sg_d = [Dep(), Dep()]
            tt_d = [Dep(), Dep()]
            pT_d = Dep()
            wpp_d = [Dep(), Dep()]
            op("pool", lambda E: E.dma_start(out=pTt[:], in_=pT_in[l].rearrange("(j p) t -> p j t", p=128)),
               W=[pT_d], dma=pm_sem)
            for b in range(NB):
                k = b % 2
                op("pool", lambda E, b=b, k=k: E.tensor_copy(out=xb16[k][:], in_=ybuf[:, b, :]),
                   R=[y_d[b]], W=[xb_d[k]])
                transpose_block(xb16[k], xb_d[k], D, A, A_d, b)
            i = 0
            state["stg"] = (B.bitcast(F32)[:, :, :].rearrange("p c n -> p (c n)"), B_d)
            for n in range(4):
                wt, wd = load_w(w_pg[l][:, n * 512:(n + 1) * 512], KC, 512, key=("pg", l, n))
                kk = n % 2
                op("pool", lambda E, n=n, kk=kk: E.dma_start(
                    out=wppt[kk][:], in_=w_pp[l][:, n * 512:(n + 1) * 512].rearrange("(j p) n -> p j n", p=128)),
                   W=[wpp_d[kk]], dma=ld_sem[2 + kk])
                for b in range(NB):
                    k = i % 2
                    i += 1
                    pa, pa_d = next_ps()
                    for c in range(KC):
                        op("pe", lambda E, c=c, pa=pa, wt=wt, b=b: E.matmul(
                            pa[:], lhsT=A[:, c, b * 128:(b + 1) * 128], rhs=wt[:, c, :],
                            start=(c == 0), stop=(c == KC - 1)), R=[wd, A_d], W=[pa_d])
                    pb, pb_d = next_ps()
                    for j in range(2):
                        op("pe", lambda E, j=j, pb=pb, b=b, kk=kk: E.matmul(
                            pb[:], lhsT=pTt[:, j, b * 128:(b + 1) * 128], rhs=wppt[kk][:, j, :],
                            start=(j == 0), stop=(j == 1)), R=[pT_d, wpp_d[kk]], W=[pb_d])
                    op("act", lambda E, pa=pa, k=k: E.activation(out=sgt[k][:], in_=pa[:], func=AF.Sigmoid),
                       R=[pa_d], W=[sg_d[k]])
                    op("dve", lambda E, pb=pb, k=k: E.tensor_tensor(out=tt[k][:], in0=pb[:], in1=sgt[k][:],
                                                                    op=ALU.mult), R=[pb_d, sg_d[k]], W=[tt_d[k]])
                    op("pool", lambda E, b=b, n=n, k=k: E.tensor_tensor(
                        out=ybuf[:, b, n * 512:(n + 1) * 512], in0=ybuf[:, b, n * 512:(n + 1) * 512],
                        in1=tt[k][:], op=ALU.add), R=[y_d[b], tt_d[k]], W=[y_d[b]])

        def store_rows(dst, ybuf, y_d, sem, dst_d=None):
            state["stg"] = None
            for b in range(NB):
                op("sp", lambda E, b=b: E.dma_start(out=dst[b * 128:(b + 1) * 128, :], in_=ybuf[:, b, :]),
                   R=[y_d[b]], W=[dst_d] if dst_d is not None else [], dma=sem)

        def latent_norm_T(es, tag, wsrc, src, src_d, gvec, dst, dst_d):
            def al(name, shape, dt):
                return es.enter_context(nc.sbuf_tensor(name, shape, dt))
            gl = al("gl" + tag, [128, 512], F32)
            cn = [al("cn%s%d" % (tag, i), [128, 512], BF16) for i in range(2)]
            jk = al("jk" + tag, [128, 512], BF16)
            gl_d, jk_d = Dep(), Dep()
            cn_d = [Dep(), Dep()]
            load_bcast(gl, gvec, gl_d)
            wt, wd = load_w(wsrc, KC, 512, key=("lat", tag))
            for b in range(NB):
                k = b % 2
                p, p_d = next_ps()
                for c in range(KC):
                    op("pe", lambda E, c=c, p=p, b=b: E.matmul(
                        p[:], lhsT=src[:, c, b * 128:(b + 1) * 128], rhs=wt[:, c, :],
                        start=(c == 0), stop=(c == KC - 1)), R=[wd, src_d], W=[p_d])
                ssq, ssq_d = stat_col()
                op("act", lambda E, p=p, ssq=ssq: E.activation(out=jk[:], in_=p[:], func=AF.Square, accum_out=ssq),
                   R=[p_d], W=[jk_d, ssq_d])
                r, r_d = rstd_from_ssq(ssq, ssq_d, 512)
                op("dve", lambda E, p=p, r=r, k=k: E.scalar_tensor_tensor(
                    out=cn[k][:], in0=p[:], scalar=r, in1=gl[:], op0=ALU.mult, op1=ALU.mult),
                   R=[p_d, r_d, gl_d], W=[cn_d[k]])
                transpose_block(cn[k], cn_d[k], 512, dst, dst_d, b)

        def rope_fm(wt, wd, kc, colA, wsw, wsw_d, src, src_d, cosT, sinT, cs_d, t1, t1_d, t2, t2_d, dst, dst_d):
            for half in range(2):
                pa, pa_d = next_ps()
                pb, pb_d = next_ps()
                for c in range(kc):
                    op("pe", lambda E, c=c, pa=pa, half=half: E.matmul(
                        pa[0:64, :], lhsT=wt[:, c, colA:colA + 64], rhs=src[:, c, half * 512:(half + 1) * 512],
                        start=(c == 0), stop=(c == kc - 1)), R=[wd, src_d], W=[pa_d])
                for c in range(kc):
                    op("pe", lambda E, c=c, pb=pb, half=half: E.matmul(
                        pb[0:64, :], lhsT=wsw[:, c, 0:64], rhs=src[:, c, half * 512:(half + 1) * 512],
                        start=(c == 0), stop=(c == kc - 1)), R=[wsw_d, src_d], W=[pb_d])
                hs = slice(half * 512, (half + 1) * 512)
                op("dve", lambda E, pa=pa, hs=hs: E.tensor_tensor(out=t1[0:64, :], in0=pa[0:64, :],
                                                                  in1=cosT[0:64, hs], op=ALU.mult),
                   R=[pa_d, cs_d], W=[t1_d])
                op("dve", lambda E, pb=pb, hs=hs: E.tensor_tensor(out=t2[0:64, :], in0=pb[0:64, :],
                                                                  in1=sinT[0:64, hs], op=ALU.mult),
                   R=[pb_d, cs_d], W=[t2_d])
                op("pool", lambda E, hs=hs: E.tensor_tensor(out=dst[0:64, hs], in0=t1[0:64, :], in1=t2[0:64, :],
                                                            op=ALU.add), R=[t1_d, t2_d], W=[dst_d])

        def build_swap(wsw, wsw_d, wt, wd, kc, colA):
            op("pool", lambda E: E.tensor_copy(out=wsw[:, 0:kc, 0:32], in_=wt[:, :, colA + 32:colA + 64]),
               R=[wd], W=[wsw_d])
            op("pool", lambda E: E.tensor_copy(out=wsw[:, 0:kc, 32:64], in_=wt[:, :, colA:colA + 32]),
               R=[wd], W=[wsw_d])

        def P6a(ybuf, y_d, es):
            def al(name, shape, dt):
                return es.enter_context(nc.sbuf_tensor(name, shape, dt))
            junk = al("junk6", [128, D], BF16)
            xn = [al("xn6_%d" % i, [128, D], BF16) for i in range(2)]
            gkv = al("gkv", [128, D], F32)
            g1 = al("g1", [128, D], F32)
            gkv_d, g1_d = Dep(), Dep()
            load_bcast(gkv, kv_norm, gkv_d)
            load_bcast(g1, norm_pre[1], g1_d)

            def get_x(b):
                return ybuf[:, b, :], y_d[b]
            norm_transpose(get_x, [(gkv, gkv_d), (g1, g1_d)], [(B, B_d), (A, A_d)], junk, Dep(), xn, [Dep(), Dep()])

        def P6b(es):
            def al(name, shape, dt):
                return es.enter_context(nc.sbuf_tensor(name, shape, dt))
            ckvT = al("ckvT", [128, 4, T], BF16)
            kst = [al("kst6_%d" % i, [128, T], BF16) for i in range(2)]
            vst = [al("vst6_%d" % i, [128, 512], BF16) for i in range(2)]
            wr = al("wr6", [128, KC, 64], BF16)
            wsw = al("wsw6", [128, KC, 64], BF16)
            cosT = al("cos6", [64, T], F32)
            sinT = al("sin6", [64, T], F32)
            t1 = al("t1_6", [64, 512], F32)
            t2 = al("t2_6", [64, 512], F32)
            krT = al("krT6", [64, T], BF16)
            ckvT_d, wr_d, wsw_d, cs_d = Dep(), Dep(), Dep(), Dep()
            t1_d, t2_d, krT_d = Dep(), Dep(), Dep()
            op("sp", lambda E: E.dma_start(out=cosT[:], in_=cosT_in), W=[cs_d], dma=misc_sem)
            op("sp", lambda E: E.dma_start(out=sinT[:], in_=sinT_in), W=[cs_d], dma=misc_sem)
            op("pool", lambda E: E.dma_start(out=wr[:], in_=w_dkv[:, 512:576].rearrange("(c p) n -> p c n", p=128)),
               W=[wr_d], dma=ld_sem[2])
            build_swap(wsw, wsw_d, wr, wr_d, KC, 0)
            latent_norm_T(es, "kv", w_dkv[:, 0:512], B, B_d, kv_lat_norm, ckvT, ckvT_d)
            rope_fm(wr, wr_d, KC, 0, wsw, wsw_d, B, B_d, cosT, sinT, cs_d, t1, t1_d, t2, t2_d, krT, krT_d)
            kr_in_d = Dep()
            op("sp", lambda E: E.dma_start(out=krin, in_=krT[:]), R=[krT_d], W=[kr_in_d], dma=st_sem[0])
            gather(krin, krall, [kr_in_d], krall_d)
            kst_d = [Dep(), Dep()]
            vst_d = [Dep(), Dep()]
            kin_d = [Dep() for _ in range(H)]
            vin_d = [Dep() for _ in range(32)]
            i = 0
            steps = []
            for cg in range(4):
                steps.append(("k", cg, w_uk[:, cg * 512:(cg + 1) * 512]))
                steps.append(("v", cg, w_uv[:, cg * 512:(cg + 1) * 512]))
            loaded = {0: load_w(steps[0][2], 4, 512)}
            for si, (kind, cg, _) in enumerate(steps):
                wt, wd = loaded.pop(si)
                if si + 1 < len(steps):
                    loaded[si + 1] = load_w(steps[si + 1][2], 4, 512)
                if kind == "k":
                    for hh in range(4):
                        h = cg * 4 + hh
                        k = h % 2
                        proj_fm(wt, wd, 4, hh * 128, 128, ckvT, ckvT_d, kst[k], kst_d[k])
                        op("sp", lambda E, hh=hh, cg=cg, k=k: E.dma_start(
                            out=kin[1][cg][hh * 128:(hh + 1) * 128, :], in_=kst[k][:]),
                           R=[kst_d[k]], W=[kin_d[h]], dma=st_sem[k])
                    gather(kin[1][cg], kall[1][cg], kin_d[cg * 4:cg * 4 + 4], kall_d[1][cg],
                           defer=(1, cg) if cg > 0 else None)
                else:
                    for b in range(NB):
                        k = i % 2
                        i += 1
                        p, p_d = next_ps()
                        for c in range(4):
                            op("pe", lambda E, c=c, p=p, wt=wt, b=b: E.matmul(
                                p[:], lhsT=ckvT[:, c, b * 128:(b + 1) * 128], rhs=wt[:, c, :],
                                start=(c == 0), stop=(c == 3)), R=[wd, ckvT_d], W=[p_d])
                        evac(vst[k][:], p[:], [p_d], [vst_d[k]])
                        op("sp", lambda E, b=b, cg=cg, k=k: E.dma_start(
                            out=vin[1][cg][b * 128:(b + 1) * 128, :], in_=vst[k][:]),
                           R=[vst_d[k]], W=[vin_d[cg * NB + b]], dma=st_sem[2 + k])
                    gather(vin[1][cg], vall[1][cg], vin_d[cg * NB:(cg + 1) * NB], vall_d[1][cg],
                           defer=(1, cg) if cg > 0 else None)
            latent_norm_T(es, "q", w_in_b[:, 0:512], A, A_d, q_lat_norm, cqT, cqT_d)

        def P9():
          with ExitStack() as es:
            def al(name, shape, dt):
                return es.enter_context(nc.sbuf_tensor(name, shape, dt))
            kh = [al("kh9_%d" % i, [128, SEQ], BF16) for i in range(2)]
            vh = [al("vh9_%d" % i, [128, 32, 128], BF16) for i in range(2)]
            kr = al("kr9", [64, SEQ], BF16)
            scb = al("scb", [128, 9, 512], F32)
            qn = [al("qn%d" % i, [128, T], BF16) for i in range(2)]
            qr = [al("qr%d" % i, [64, T], BF16) for i in range(2)]
            gT = [al("gT9_%d" % i, [128, T], BF16) for i in range(2)]
            cosT = al("cos9", [64, T], F32)
            sinT = al("sin9", [64, T], F32)
            mb1 = al("mb1s", [128, 2, 512], F32)
            wbt = [al("wb9_%d" % i, [128, 512], BF16) for i in range(3)]
            wTt = [al("wT9_%d" % i, [128, 512], BF16) for i in range(3)]
            t1 = al("t1_9", [64, 512], F32)
            t2 = al("t2_9", [64, 512], F32)
            wsw = al("wsw9", [128, 4, 64], BF16)
            mxt = [al("mxt%d" % i, [128, 8], F32) for i in range(4)]
            rst = [al("rst%d" % i, [128, 8], F32) for i in range(4)]
            dg = [al("dg%d" % i, [128, 128], F32) for i in range(2)]
            tg = [al("tg%d" % i, [128, 128], F32) for i in range(2)]
            kh_d, vh_d, qn_d, qr_d, gT_d = ([Dep(), Dep()] for _ in range(5))
            wb_d, wT_d, mx_d, rs_d, dg_d, tg_d = ([Dep(), Dep(), Dep(), Dep()] for _ in range(6))
            kr_d, cs_d, mask_d, t1_d, t2_d, wsw_d = (Dep() for _ in range(6))
            sc_d = [Dep() for _ in range(9)]
            op("sp", lambda E: E.dma_start(out=cosT[:], in_=cosT_in), W=[cs_d], dma=misc_sem)
            op("sp", lambda E: E.dma_start(out=sinT[:], in_=sinT_in), W=[cs_d], dma=misc_sem)
            op("sp", lambda E: E.dma_start(out=mb1[:], in_=mb1_in.rearrange("k p n -> p k n")),
               W=[mask_d], dma=misc_sem)
            for r in range(4):
                for par in range(2):
                    nb0 = r if par == 0 else 7 - r
                    src = krall[r * 64:(r + 1) * 64, :].rearrange("p (s two n) -> p two s n", two=2, n=128)[:, par, :, :]
                    dst = kr[:, :].rearrange("p (s e n) -> p e s n", e=8, n=128)[:, nb0, :, :]
                    op("sp", lambda E, src=src, dst=dst: E.dma_start(out=dst, in_=src),
                       R=[krall_d], W=[kr_d], dma=kr_sem)
            hw = {}

            def head_load(h):
                hb = h % 2
                load_kv_head(1, h, kh[hb], kh_d[hb], vh[hb], vh_d[hb], ld_sem[hb])
                hw[h] = (load_w(w_uq[:, h * 192:(h + 1) * 192], 4, 192),
                         load_w(w_in_b[:, 512 + h * 128: 512 + (h + 1) * 128], KC, 128))

            def head_proj(h):
                hb = h % 2
                (wt, wd), (wt2, wd2) = hw.pop(h)
                build_swap(wsw, wsw_d, wt, wd, 4, 128)
                proj_fm(wt, wd, 4, 0, 128, cqT, cqT_d, qn[hb], qn_d[hb])
                rope_fm(wt, wd, 4, 128, wsw, wsw_d, cqT, cqT_d, cosT, sinT, cs_d, t1, t1_d, t2, t2_d,
                        qr[hb], qr_d[hb])
                proj_fm(wt2, wd2, KC, 0, 128, A, A_d, gT[hb], gT_d[hb], func=AF.Silu)

            def ph1(sl, c, r, i):
                h, s, nt, mi = sl["h"], sl["s"], sl["nt"], sl["mi"]
                hb = h % 2
                sl_ = slice(s * 128, (s + 1) * 128)
                ks = slice(c * 512, (c + 1) * 512)
                p, p_d = next_ps(0, 3)
                op("pe", lambda E: E.matmul(p[:], lhsT=qn[hb][:, sl_], rhs=kh[hb][:, ks], start=True, stop=False),
                   R=[qn_d[hb], kh_d[hb]], W=[p_d])
                op("pe", lambda E: E.matmul(p[:], lhsT=qr[hb][0:64, sl_], rhs=kr[0:64, ks], start=False, stop=True),
                   R=[qr_d[hb], kr_d], W=[p_d])
                if c == nt - 1:
                    par = s % 2
                    op("dve", lambda E: E.tensor_tensor(out=scb[:, r, :], in0=p[:], in1=mb1[:, par, :], op=ALU.add),
                       R=[p_d, mask_d], W=[sc_d[r]])
                else:
                    op("act", lambda E: E.activation(out=scb[:, r, :], in_=p[:], func=AF.Copy),
                       R=[p_d], W=[sc_d[r]])
                op("dve", lambda E: E.tensor_reduce(out=mxt[mi][:, c:c + 1], in_=scb[:, r, :], axis=AX.X, op=ALU.max),
                   R=[sc_d[r]], W=[mx_d[mi]])

            def mid(sl):
                nt, mi = sl["nt"], sl["mi"]
                m, m_d = stat_col()
                nb_, nb_d = stat_col()
                op("dve", lambda E: E.tensor_reduce(out=m, in_=mxt[mi][:, 0:nt], axis=AX.X, op=ALU.max),
                   R=[mx_d[mi]], W=[m_d])
                op("dve", lambda E: E.tensor_scalar(out=nb_, in0=m, scalar1=-SC_B, scalar2=None, op0=ALU.mult),
                   R=[m_d], W=[nb_d])
                sl["nb"] = (nb_, nb_d)
                sl["po"] = next_ps(3, 5)

            def stC(sl, c, r, i):
                mi = sl["mi"]
                k3 = i % 3
                nb_, nb_d = sl["nb"]
                op("act", lambda E: E.activation(out=wbt[k3][:], in_=scb[:, r, :], func=AF.Exp, bias=nb_, scale=SC_B,
                                                 accum_out=rst[mi][:, c:c + 1]),
                   R=[sc_d[r], nb_d], W=[wb_d[k3], rs_d[mi]])

            def stD(sl, c, r, i):
                k3 = i % 3
                pv_T(wbt[k3], wb_d[k3], wTt[k3], wT_d[k3])

            def stE(sl, c, r, i):
                h, s, nt, mi = sl["h"], sl["s"], sl["nt"], sl["mi"]
                hb = h % 2
                sb = mi % 2
                sl_ = slice(s * 128, (s + 1) * 128)
                k3 = i % 3
                po, po_d = sl["po"]
                pv_mm(wTt[k3], wT_d[k3], vh[hb], vh_d[hb], po, po_d, c, nt)
                if c == nt - 1:
                    rs, rs1_d = stat_col()
                    rr, rr_d = stat_col()
                    op("dve", lambda E: E.tensor_reduce(out=rs, in_=rst[mi][:, 0:nt], axis=AX.X, op=ALU.add),
                       R=[rs_d[mi]], W=[rs1_d])
                    op("dve", lambda E: E.reciprocal(out=rr, in_=rs), R=[rs1_d], W=[rr_d])
                    op("dve", lambda E: E.tensor_scalar(out=dg[sb][:], in0=identf[:], scalar1=rr, scalar2=None,
                                                        op0=ALU.mult), R=[rr_d, const_d], W=[dg_d[sb]])
                    prb, prb_d = next_ps(5, 6)
                    op("pe", lambda E: E.matmul(prb[:, 0:128], lhsT=onesf[:], rhs=dg[sb][:], start=True, stop=True),
                       R=[dg_d[sb], const_d], W=[prb_d])
                    op("dve", lambda E: E.tensor_tensor(out=tg[sb][:], in0=prb[:, 0:128], in1=gT[hb][:, sl_],
                                                        op=ALU.mult), R=[prb_d, gT_d[hb]], W=[tg_d[sb]])
                    op("dve", lambda E: E.tensor_tensor(out=B[:, h, sl_], in0=po[:, 0:128], in1=tg[sb][:],
                                                        op=ALU.mult), R=[po_d, tg_d[sb]], W=[B_d])

            def run_round(cur, prev):
                ct = cur["tiles"] if cur else []
                pt_ = prev["tiles"] if prev else []
                if prev is not None:
                    for sl in prev["slots"]:
                        mid(sl)
                n = max(len(ct), len(pt_))
                for t in range(-1, n + 2):
                    if 0 <= t + 1 < len(pt_):
                        stC(*pt_[t + 1])
                    if 0 <= t < len(ct):
                        ph1(*ct[t])
                    if 0 <= t < len(pt_):
                        stD(*pt_[t])
                    if 0 <= t - 1 < len(pt_):
                        stE(*pt_[t - 1])

            prev = None
            head_load(0)
            head_proj(0)
            gi = 0
            rnd = 0
            for h in range(H):
                for ri, pair in enumerate(((0, 7), (1, 6), (2, 5), (3, 4))):
                    if h in (0, 2, 4) and ri == 2:
                        emit_deferred(1, 1 + h // 2, 0)
                    if h in (1, 3, 5) and ri == 0:
                        emit_deferred(1, 1 + h // 2, 1)
                    if ri == 1 and h + 1 < H:
                        head_load(h + 1)
                    if ri == 3 and h + 1 < H:
                        head_proj(h + 1)
                    if ri == 0 and h == H - 1:
                        prefetch_w(("out", 1, 0), w_out_b[:, 0:512], KC, 512)
                        prefetch_w(("out", 1, 1), w_out_b[:, 512:1024], KC, 512)
                    slots = []
                    r = 0
                    for si_, s_ in enumerate(pair):
                        slots.append({"h": h, "s": s_, "nt": s_ + 1, "r0": r, "mi": (rnd % 2) * 2 + si_})
                        r += s_ + 1
                    tiles = []
                    for sl in slots:
                        for c in range(sl["nt"]):
                            tiles.append((sl, c, sl["r0"] + c, gi))
                            gi += 1
                    cur = {"slots": slots, "tiles": tiles}
                    run_round(cur, prev)
                    prev = cur
                    rnd += 1
            run_round(None, prev)
            sc.end_phase()

        def dump(src_rows_fn):
            pass

        cqT_d = Dep()
        P01()
        if upto >= 3:
            P3()
        if upto >= 4:
            with ExitStack() as es0:
                ybuf = es0.enter_context(nc.sbuf_tensor("ybuf0", [128, NB, D], F32))
                y_d = [Dep() for _ in range(NB)]
                with ExitStack() as es:
                    P_out(0, w_out_a, ybuf, y_d, x_own, es)
                    sc.end_phase()
                if upto >= 5:
                    with ExitStack() as es:
                        P_ple(0, ybuf, y_d, es)
                        prefetch_w(("lat", "kv"), w_dkv[:, 0:512], KC, 512)
                        xs_d = Dep()
                        store_rows(xs, ybuf, y_d, st_sem[1], xs_d)
                        if debug and upto == 5:
                            store_rows(dbg, ybuf, y_d, st_sem[2])
                        sc.end_phase()
                elif debug:
                    store_rows(dbg, ybuf, y_d, st_sem[2])
                    sc.end_phase()
                if upto >= 6:
                    with ExitStack() as es:
                        P6a(ybuf, y_d, es)
                        sc.end_phase()
        if upto >= 6:
            with ExitStack() as es:
                P6b(es)
                sc.end_phase()
        if upto >= 9:
            P9()
        if upto >= 10:
            with ExitStack() as es0:
                ybuf = es0.enter_context(nc.sbuf_tensor("ybuf1", [128, NB, D], F32))
                y_d = [Dep() for _ in range(NB)]
                with ExitStack() as es:
                    P_out(1, w_out_b, ybuf, y_d, xs, es)
                    sc.end_phase()
                with ExitStack() as es:
                    P_ple(1, ybuf, y_d, es)
                    store_rows(out_own, ybuf, y_d, st_sem[1])
                    sc.end_phase()
        elif not debug or upto < 4:
            pass
        sc.end_phase()
        print("instructions issued:", sc.ninst, {e: sc.prog[e].cnt for e in sc.ENG})
    return nc


def host_prep(inputs):
    x = np.ascontiguousarray(inputs["x"], dtype=np.float32)
    p = np.ascontiguousarray(inputs["p"], dtype=np.float32)
    in_maps = []
    inv = (1.0 / (10000.0 ** (np.arange(0, 64, 2, dtype=np.float32) / np.float32(64)))).astype(np.float32)
    jj = np.arange(512)[None, :]
    ii = np.arange(128)[:, None]
    ident = np.eye(128, dtype=np.float32)
    shared = {}
    for k in ("norm_pre", "norm_post", "kv_norm", "w_dkv", "kv_latent_norm", "w_uk", "w_uv",
              "w_ple_proj", "w_ple_gate"):
        shared[k] = np.ascontiguousarray(inputs[k], dtype=np.float32)
    for k in ("w_in_a", "w_out_a", "w_in_b", "q_latent_norm", "w_uq", "w_out_b"):
        shared[k] = np.ascontiguousarray(inputs[k][0], dtype=np.float32)
    shared["ident"] = ident
    toks = []
    for c in range(8):
        b, r = c // 4, c % 4
        blocks = [qblock(r, s) for s in range(NB)]
        tok = np.concatenate([np.arange(q * 128, (q + 1) * 128) for q in blocks])
        toks.append((b, tok))
        m = dict(shared)
        m["x_own"] = np.ascontiguousarray(x[b, tok, :])
        m["pT"] = np.ascontiguousarray(np.transpose(p[:, b, tok, :], (0, 2, 1)))
        m01 = np.zeros((2, 128, 512), np.float32)
        mb1 = np.zeros((2, 128, 512), np.float32)
        for par in range(2):
            o = r if par == 0 else 3 - r
            m01[par] = (jj < o * 128 + ii).astype(np.float32)
            mb1[par] = np.where((jj // 64) <= ((o * 128 + ii) // 64), 0.0, -1.0e5).astype(np.float32)
        m["m01"] = m01
        m["mneg"] = ((m01 - 1.0) * 30000.0).astype(np.float32)
        m["mb1"] = mb1
        inv64 = 1.0 / (10000.0 ** (np.arange(0, 64, 2, dtype=np.float64) / 64.0))
        ang = tok.astype(np.float64)[:, None] * inv64[None, :]
        cos = np.cos(ang).astype(np.float32).T
        sin = np.sin(ang).astype(np.float32).T
        m["cosT"] = np.ascontiguousarray(np.concatenate([cos, cos], 0))
        m["sinT"] = np.ascontiguousarray(np.concatenate([-sin, sin], 0))
        in_maps.append(m)
    return in_maps, toks


_CACHE = {}


def kernel(**inputs):
    in_maps, toks = host_prep(inputs)
    if "nc" not in _CACHE:
        _CACHE["nc"] = build_program()
    nc = _CACHE["nc"]
    res = run_bass_kernel_spmd(nc, in_maps, core_ids=list(range(8)))
    out = np.empty((2, SEQ, D), np.float32)
    for c in range(8):
        b, tok = toks[c]
        out[b, tok, :] = res.results[c]["out_own"]
    return out
```
